# Optimizing a Trainium2 kernel written in Bass

```python
import jax, jax.numpy as jnp
from jax import lax
import numpy as np


D_MODEL = 1024
BATCH = 4
SEQ = 4096
DEPTH = 1

EPS = 1e-6
D_FF = 2816
FFN_RES_WEIGHT = 0.5
CONV_DIM = D_MODEL
CONV_K = 3
GLA_HEADS = 4
GLA_DK = D_MODEL // 2 // GLA_HEADS
GLA_DV = D_MODEL // GLA_HEADS
GLA_QK = GLA_HEADS * GLA_DK
GLA_V = GLA_HEADS * GLA_DV
GATE_RANK = 16
GATE_TAU = 16.0
CHUNK = 64
IN_SIZES = (CONV_DIM, CONV_DIM, CONV_DIM, GLA_QK, GLA_QK, GLA_V, GATE_RANK, GLA_V, D_MODEL, D_MODEL)
IN_COLS = sum(IN_SIZES)
SPLIT_POINTS = tuple(int(s) for s in np.cumsum(IN_SIZES)[:-1])

kernel_name = 'hybrid_conv_gla_macaron_block'


def rmsnorm(x, g):
    xf = x.astype(jnp.float32)
    y = xf * lax.rsqrt(jnp.mean(xf * xf, axis=-1, keepdims=True) + EPS)
    return (y * g.astype(jnp.float32)).astype(x.dtype)


def swiglu(x, w_gu, w_down):
    g, u = jnp.split(x @ w_gu, 2, axis=-1)
    return (jax.nn.silu(g) * u) @ w_down


def causal_short_conv(u, w):
    S = u.shape[1]
    up = jnp.pad(u, ((0, 0), (CONV_K - 1, 0), (0, 0)))
    out = up[:, 0:S] * w[0]
    for j in range(1, CONV_K):
        out = out + up[:, j:j + S] * w[j]
    return out


def gla_chunked(q, k, v, log_a):
    B, S = q.shape[0], q.shape[1]
    N = S // CHUNK
    f32 = jnp.float32

    def to_chunks(t):
        return t.reshape(B, N, CHUNK, GLA_HEADS, t.shape[-1]).transpose(0, 3, 1, 2, 4).astype(f32)

    qc = to_chunks(q) * (GLA_DK ** -0.5)
    kc, vc, gc = to_chunks(k), to_chunks(v), to_chunks(log_a)
    b = jnp.cumsum(gc, axis=3)
    b_last = b[:, :, :, -1:]
    b_ref = b[:, :, :, CHUNK // 2 - 1:CHUNK // 2]

    q_ref = qc * jnp.exp(b - b_ref)
    k_ref = kc * jnp.exp(b_ref - b)
    scores = jnp.einsum('bhnid,bhnjd->bhnij', q_ref, k_ref)
    causal = jnp.tril(jnp.ones((CHUNK, CHUNK), dtype=bool))
    scores = jnp.where(causal, scores, 0.0)
    o_intra = jnp.einsum('bhnij,bhnjv->bhniv', scores, vc)

    q_inter = q_ref * jnp.exp(b_ref)
    k_state = kc * jnp.exp(b_last - b)
    kv = jnp.einsum('bhncd,bhncv->nbhdv', k_state, vc)
    decay = jnp.exp(b_last[:, :, :, 0]).transpose(2, 0, 1, 3)

    def step(state, inp):
        dec, kv_n = inp
        return state * dec[..., None] + kv_n, state

    init = jnp.zeros((B, GLA_HEADS, GLA_DK, GLA_DV), f32)
    _, states = lax.scan(step, init, (decay, kv))
    o_inter = jnp.einsum('bhncd,nbhdv->bhncv', q_inter, states)
    o = o_intra + o_inter
    return o.transpose(0, 2, 3, 1, 4).reshape(B, S, GLA_HEADS, GLA_DV)


def hybrid_mixer(h, w_in, conv_w, gate_w_up, gate_b, gla_norm_g, w_out_conv, w_out_gla, w_o):
    B, S, _ = h.shape
    z = h @ w_in
    cb, cc, cu, q, k, v, g_lr, r, ga, gb = jnp.split(z, SPLIT_POINTS, axis=-1)

    y_conv = (cb * causal_short_conv(cc * cu, conv_w)) @ w_out_conv

    log_a = jax.nn.log_sigmoid((g_lr @ gate_w_up + gate_b).astype(jnp.float32)) / GATE_TAU
    o = gla_chunked(q.reshape(B, S, GLA_HEADS, GLA_DK),
                    k.reshape(B, S, GLA_HEADS, GLA_DK),
                    v.reshape(B, S, GLA_HEADS, GLA_DV),
                    log_a.reshape(B, S, GLA_HEADS, GLA_DK))
    o = o * lax.rsqrt(jnp.mean(o * o, axis=-1, keepdims=True) + EPS)
    o = o.reshape(B, S, GLA_V) * gla_norm_g.astype(jnp.float32)
    y_gla = (o.astype(h.dtype) * jax.nn.silu(r)) @ w_out_gla

    merged = jax.nn.sigmoid(ga) * y_conv + jax.nn.sigmoid(gb) * y_gla
    return merged @ w_o


def setup_inputs(seed: int = 0) -> dict:
    key = jax.random.key(seed)
    ks = jax.random.split(key, 20)
    f32 = jnp.float32

    def nrm(k, shape, fan_in):
        return jax.random.normal(k, shape, f32) * (fan_in ** -0.5)

    def gain(k, shape):
        return 1.0 + 0.02 * jax.random.normal(k, shape, f32)

    L = DEPTH
    return {
        'x': jax.random.normal(ks[0], (BATCH, SEQ, D_MODEL), f32),
        'norm_ffn1_g': gain(ks[1], (L, D_MODEL)),
        'ffn1_w_gu': nrm(ks[2], (L, D_MODEL, 2 * D_FF), D_MODEL),
        'ffn1_w_down': nrm(ks[3], (L, D_FF, D_MODEL), D_FF),
        'norm_mix_g': gain(ks[4], (L, D_MODEL)),
        'w_in': nrm(ks[5], (L, D_MODEL, IN_COLS), D_MODEL),
        'conv_w': nrm(ks[6], (L, CONV_K, CONV_DIM), CONV_K),
        'gate_w_up': nrm(ks[7], (L, GATE_RANK, GLA_QK), GATE_RANK),
        'gate_b': 0.1 * jax.random.normal(ks[8], (L, GLA_QK), f32),
        'gla_norm_g': gain(ks[9], (L, GLA_V)),
        'w_out_conv': nrm(ks[10], (L, CONV_DIM, D_MODEL), CONV_DIM),
        'w_out_gla': nrm(ks[11], (L, GLA_V, D_MODEL), GLA_V),
        'w_o': nrm(ks[12], (L, D_MODEL, D_MODEL), D_MODEL),
        'norm_ffn2_g': gain(ks[13], (L, D_MODEL)),
        'ffn2_w_gu': nrm(ks[14], (L, D_MODEL, 2 * D_FF), D_MODEL),
        'ffn2_w_down': nrm(ks[15], (L, D_FF, D_MODEL), D_FF),
        'final_norm_g': gain(ks[16], (D_MODEL,)),
    }


def reference(x, norm_ffn1_g, ffn1_w_gu, ffn1_w_down, norm_mix_g, w_in, conv_w, gate_w_up, gate_b,
              gla_norm_g, w_out_conv, w_out_gla, w_o, norm_ffn2_g, ffn2_w_gu, ffn2_w_down, final_norm_g):
    for l in range(DEPTH):
        x = x + FFN_RES_WEIGHT * swiglu(rmsnorm(x, norm_ffn1_g[l]), ffn1_w_gu[l], ffn1_w_down[l])
        x = x + hybrid_mixer(rmsnorm(x, norm_mix_g[l]), w_in[l], conv_w[l], gate_w_up[l], gate_b[l],
                             gla_norm_g[l], w_out_conv[l], w_out_gla[l], w_o[l])
        x = x + FFN_RES_WEIGHT * swiglu(rmsnorm(x, norm_ffn2_g[l]), ffn2_w_gu[l], ffn2_w_down[l])
    return rmsnorm(x, final_norm_g)
```

```python
import os
import numpy as np
import concourse.bass as bass
import concourse.mybir as mybir
from concourse.bass_utils import run_bass_kernel_spmd

F32 = mybir.dt.float32
BF16 = mybir.dt.bfloat16
AF = mybir.ActivationFunctionType
ALU = mybir.AluOpType

D = 1024
DFF = 2816
NCH = 22
T = 2048
NTB = 16
NTT = 4
EPS = 1e-6
INC = 8208
O_CB, O_CC, O_CU, O_Q, O_K, O_V, O_G, O_R, O_GA, O_GB = 0, 1024, 2048, 3072, 3584, 4096, 5120, 5136, 6160, 7184

STAGE = int(os.environ.get("KSTAGE", "4"))
EXCH = int(os.environ.get("KEXCH", "1"))
DBG = int(os.environ.get("KDBG", "0"))

ENGS = ["pe", "act", "dve", "pool", "sp"]


class Res:
    __slots__ = ("name", "w", "r", "parent")

    def __init__(self, name):
        self.name = name
        self.w = None
        self.r = {}
        self.parent = None


class Op:
    __slots__ = ("eng", "fn", "deps", "sig", "dma", "key", "val", "inc")


class Sched:
    def __init__(self):
        self.streams = {e: [] for e in ENGS}
        self.barrier_deps = {}

    def add(self, eng, fn, reads=(), writes=(), dma=False, key=None, inc=16):
        op = Op()
        op.eng, op.fn, op.dma, op.key, op.sig, op.deps, op.val, op.inc = eng, fn, dma, key, dma, set(), 0, inc

        def consider(d, kind):
            if d is None or d is op:
                return
            if d.dma or dma:
                op.deps.add(d)
                return
            if d.eng == eng:
                if eng == "pe":
                    return
            op.deps.add(d)

        for r in reads:
            consider(r.w, "raw")
            if r.name.startswith("('ps'"):
                for d in r.r.values():
                    if d.eng != eng:
                        consider(d, "raw")
        for w in writes:
            consider(w.w, "waw")
            for d in w.r.values():
                consider(d, "war")
            if w.parent is not None:
                for d in w.parent.r.values():
                    consider(d, "war")
        if eng in self.barrier_deps:
            for d in self.barrier_deps.pop(eng):
                consider(d, "raw")
        rk = ("dma", key) if dma else eng
        for r in reads:
            r.r[rk] = op
            if r.parent is not None:
                r.parent.r[rk] = op
        for w in writes:
            w.w = op
            w.r = {}
        self.streams[eng].append(op)
        return op

    def barrier(self):
        last = [s[-1] for s in self.streams.values() if s]
        for e in ENGS:
            self.barrier_deps[e] = list(last)

    def finalize(self):
        for s in self.streams.values():
            for op in s:
                for d in op.deps:
                    d.sig = True
        keycnt = {}
        for e, s in self.streams.items():
            c = 0
            for op in s:
                if op.dma:
                    keycnt[op.key] = keycnt.get(op.key, 0) + op.inc
                    op.val = keycnt[op.key]
                elif op.sig:
                    c += 1
                    op.val = c
        return sorted(keycnt.keys(), key=str)


SLOTKEYS = {}
NSLOTS = 44


def build_nc():
    SLOTKEYS.clear()
    nc = bass.Bass("TRN2", target_bir_lowering=False)

    def din(name, shape, dt=F32):
        return nc.dram_tensor(name, list(shape), dt, kind="ExternalInput").ap()

    x_d = din("x", [T, D])
    out_d = nc.dram_tensor("out", [T, D], F32, kind="ExternalOutput").ap()
    g_d = [din(n, [128, D]) for n in ("g1", "gm", "g2", "gf")]
    wdn_d = [din("wdn1", [DFF, D]), din("wdn2", [DFF, D])]
    wglr_d = din("wglr_in", [D, 16])
    wslots_d = din("wslots", [NSLOTS, 128, 4096])
    wsb_d = nc.dram_tensor("wsb", [NSLOTS, 128, 4096], BF16).ap()
    convw_d = din("convw", [128, 8, 3])
    glag_d = din("glag", [128, 8])
    gwu_d = din("gwu", [17, 512])
    flag_d = din("flag", [128, 1])
    ident_d = din("ident", [128, 128])
    cmat_d = din("cmat", [128, 512])
    lst_d = din("lstrict", [128, 128])
    maskT_d = din("maskT", [128, 128])
    onesm_d = din("onesm", [128, 128])
    XW = 4 * 256 + 16
    if EXCH:
        xsend_d = nc.dram_tensor("xsend", [128, 1024], F32).ap()
        xrecv_d = nc.dram_tensor("xrecv", [256, 1024], F32).ap()
        hsend_d = nc.dram_tensor("hsend", [128, 16], F32).ap()
        hrecv_d = nc.dram_tensor("hrecv", [256, 16], F32).ap()

    S = Sched()
    es = []

    def sb(name, shape, dt):
        cm = nc.sbuf_tensor(name, list(shape), dt)
        t = cm.__enter__()
        es.append(cm)
        return t

    def pst(name):
        cm = nc.psum_tensor(name, [128, 512], F32)
        t = cm.__enter__()
        es.append(cm)
        return t

    x_sb = sb("x_sb", [128, NTB, D], F32)
    hT = sb("hT", [128, 8, T], BF16)
    ARN = 37184
    arena = sb("arena", [128, ARN], BF16)
    slots = [sb(f"slot{i}", [128, 8, 512], BF16) for i in range(3)]
    ss = sb("ss", [128, NTB], F32)
    rstd = sb("rstd", [128, NTB], F32)
    identf = sb("identf", [128, 128], F32)
    identb = sb("identb", [128, 128], BF16)
    cmat = sb("cmat_sb", [128, 512], F32)
    lstr = sb("lstr_sb", [128, 128], F32)
    maskT = sb("maskT_sb", [128, 128], F32)
    onesf = sb("onesf", [128, 128], F32)
    onesb = sb("onesb", [128, 128], BF16)
    convw = sb("convw_sb", [128, 8, 3], F32)
    glag = sb("glag_sb", [128, 8], F32)
    gwu = sb("gwu_sb", [32, 512], F32)
    flag = sb("flag_sb", [128, 1], F32)
    wglr = sb("wglr", [128, 8, 16], BF16)
    Sst = sb("Sst", [128, 4, 256], F32)
    Sbf = sb("Sbf", [128, 4, 256], BF16)
    zerob = sb("zerob", [128, 128], BF16)
    phalo = sb("phalo", [128, 8, 2], F32)
    halo_in = sb("halo_in", [128, 8, 2], F32)
    epsc = sb("epsc", [128, 2], F32)
    ps = [pst(f"ps{i}") for i in range(8)]

    def abf(off, n):
        return arena[:, off:off + n]

    def af32(off, n):
        return arena[:, off:off + 2 * n].bitcast(F32)

    actT = abf(0, 6 * T).rearrange("p (j t) -> p j t", j=6)
    wdn_sb = abf(6 * T, 6 * D).rearrange("p (j c) -> p j c", j=6)
    sact = [abf(6 * T + 6 * D + i * 512, 512) for i in range(2)]
    ostage = [af32(25600 + i * 2048, 1024) for i in range(2)]
    junk = abf(20480, 1024)
    gbuf = af32(21504, 1024)
    hnb = [abf(23552, 1024), abf(24576, 1024)]
    o = 0

    def take_bf(n):
        nonlocal o
        v = abf(o, n)
        o += n
        return v

    def take_f32(n):
        nonlocal o
        v = af32(o, n)
        o += 2 * n
        return v

    BUF1 = take_bf(8 * 512).rearrange("p (k t) -> p k t", k=8)
    BUF2 = take_bf(8 * 512).rearrange("p (k t) -> p k t", k=8)
    pbuf = [take_f32(514) for _ in range(2)]
    ccs = [take_f32(512) for _ in range(1)]
    sga = [take_f32(512) for _ in range(2)]
    TSC_OFF = o
    tsc = [take_f32(512) for _ in range(2)]
    qT = [[take_bf(512) for _ in range(2)] for _ in range(2)]
    kT = [[take_bf(512) for _ in range(2)] for _ in range(2)]
    srall = take_bf(2048)
    sr = [[srall[:, (hh * 2 + vh) * 512:(hh * 2 + vh + 1) * 512] for vh in range(2)] for hh in range(2)]
    srall3 = srall.rearrange("p (a t) -> p a t", a=4)
    NV, NE, NQI, NTR, NST, NDEC = 4, 3, 3, 2, 3, 6
    v_sb = [take_bf(512) for _ in range(NV)]
    e1 = [take_f32(256) for _ in range(2)]
    spb = [take_f32(256) for _ in range(2)]
    Eb = [take_f32(512) for _ in range(NE)]
    decb = [take_f32(4).rearrange("p (h c) -> p h c", h=2) for _ in range(NDEC)]
    qqb = [take_bf(512).rearrange("p (h a t) -> p h a t", h=2, a=2) for _ in range(NQI)]
    kkb = [take_bf(512).rearrange("p (h a t) -> p h a t", h=2, a=2) for _ in range(NTR)]
    sTb = [take_bf(256).rearrange("p (h t) -> p h t", h=2) for _ in range(NST)]
    ksT = [take_bf(256).rearrange("p (h t) -> p h t", h=2) for _ in range(NST)]
    o_sb = [take_f32(512) for _ in range(2)]
    sq = take_bf(512)
    rs = take_f32(256)
    glrT = arena[0:32, o:o + 1024].bitcast(F32)
    o += 1024
    assert o <= ARN, o

    R = {}

    def res(*k):
        if k not in R:
            R[k] = Res(str(k))
            if k[0] == "slot" and k[2] != "all":
                R[k].parent = res("slot", k[1], "all")
        return R[k]

    def mm(out, lhsT, rhs, start, stop, reads, writes):
        S.add("pe", lambda e, out=out, lhsT=lhsT, rhs=rhs, start=start, stop=stop:
              e.matmul(out, lhsT, rhs, start=start, stop=stop), reads=reads, writes=writes)

    def act(out, in_, func, reads, writes, **kw):
        S.add("act", lambda e, out=out, in_=in_, func=func, kw=kw: e.activation(out=out, in_=in_, func=func, **kw),
              reads=reads, writes=writes)

    def dve(method, reads, writes, *a, **kw):
        S.add("dve", lambda e, method=method, a=a, kw=kw: getattr(e, method)(*a, **kw), reads=reads, writes=writes)

    def dma(q, out, in_, reads, writes, key):
        S.add(q, lambda e, out=out, in_=in_: e.dma_start(out=out, in_=in_), reads=reads, writes=writes,
              dma=True, key=key)

    def load_const(dst, src, name, q="sp"):
        dma(q, dst, src, [], [res(name)], key=name)

    load_const(identf[:], ident_d, "identf")
    load_const(cmat[:], cmat_d, "cmat")
    load_const(lstr[:], lst_d, "lstr")
    load_const(maskT[:], maskT_d, "maskT")
    load_const(onesf[:], onesm_d, "onesf")
    load_const(convw[:], convw_d, "convw")
    load_const(glag[:], glag_d, "glag")
    load_const(flag[:], flag_d, "flag")
    dve("memset", [], [res("gwu")], gwu[:], 0.0)
    dve("memset", [], [res("epsc")], epsc[:, 0:1], EPS)
    dve("memset", [], [res("epsc")], epsc[:, 1:2], 1.0)
    load_const(gwu[0:17, :], gwu_d, "gwu")
    dve("tensor_copy", [res("identf")], [res("identb")], identb[:], identf[:])
    dve("tensor_copy", [res("onesf")], [res("onesb")], onesb[:], onesf[:])
    dve("memset", [], [res("Sst")], Sst[:], 0.0)
    dve("memset", [], [res("Sbf")], Sbf[:], 0.0)
    dve("memset", [], [res("zerob")], zerob[:], 0.0)
    dve("memset", [], [res("phalo")], phalo[:], 0.0)
    dve("memset", [], [res("halo_in")], halo_in[:], 0.0)

    dma("sp", gbuf, g_d[0], [], [res("gbuf")], key="gbuf")
    xv = x_d.rearrange("(tb p) d -> p tb d", p=128)
    for tt in range(NTT):
        dma("sp", x_sb[:, tt * 4:(tt + 1) * 4, :], xv[:, tt * 4:(tt + 1) * 4, :],
            [res("x", tt * 4 - 1)] if tt > 0 else [],
            [res("x", tb) for tb in range(tt * 4, tt * 4 + 4)], key=("xin", tt))

    gcount = [0]

    def load_g(i):
        dma("sp", gbuf, g_d[i], [], [res("gbuf")], key="gbuf")

    slot_ctr = [0]

    def next_slot():
        s = slot_ctr[0] % 3
        slot_ctr[0] += 1
        return s

    CONVERTED = set()

    def slot_bid(parts):
        key = tuple(parts)
        if key not in SLOTKEYS:
            SLOTKEYS[key] = len(SLOTKEYS)
        return SLOTKEYS[key]

    def convert_block(parts):
        bid = slot_bid(parts)
        if bid in CONVERTED:
            return
        CONVERTED.add(bid)
        dma("pool", wsb_d[bid], wslots_d[bid], [], [res("wsb", bid)], key=("cvt", bid))

    def load_slot(s, parts):
        bid = slot_bid(parts)
        if bid in CONVERTED:
            dma("pool", slots[s][:].rearrange("p k c -> p (k c)"), wsb_d[bid], [res("wsb", bid)],
                [res("slot", s, 0), res("slot", s, 1)], key=("slot", s))
            return
        dma("pool", slots[s][:].rearrange("p k c -> p (k c)"), wslots_d[bid],
            [res("x", 15)] if slot_ctr[0] <= 3 else [],
            [res("slot", s, 0), res("slot", s, 1)], key=("slot", s))

    def mixer_keys():
        ks = []
        for c0 in (0, 2, 4, 6):
            ks.append([(0, 256, "w_in", O_CC + c0 * 128), (256, 256, "w_in", O_CU + c0 * 128)])
        for hp in range(2):
            ks.append([(0, 256, "w_in", O_Q + hp * 256), (256, 256, "w_in", O_K + hp * 256)])
            ks.append([(0, 512, "w_in", O_V + hp * 512)])
        for cg in range(2):
            ks.append([(0, 512, "w_in", O_CB + cg * 512)])
        for mg in range(4):
            ks.append([(0, 256, "w_out_conv", mg * 256), (256, 256, "w_in", O_GA + mg * 256)])
        for hp in range(2):
            ks.append([(0, 512, "w_in", O_R + hp * 512)])
        for mg in range(4):
            ks.append([(0, 256, "w_out_gla", mg * 256), (256, 256, "w_in", O_GB + mg * 256)])
        for fh in range(2):
            ks.append([(0, 512, "w_o", fh * 512)])
        return ks

    def rms_stats(tbs, junk_=None, junk_res=None):
        junk_ = junk if junk_ is None else junk_
        junk_res = [res("junk")] if junk_res is None else junk_res
        for tb in tbs:
            act(junk_, x_sb[:, tb, :], AF.Square, [res("x", tb)], junk_res + [res("ss", tb)],
                accum_out=ss[:, tb:tb + 1])
            act(rstd[:, tb:tb + 1], ss[:, tb:tb + 1], AF.Ln, [res("ss", tb), res("epsc")], [res("rstd", tb)],
                scale=1.0 / D, bias=epsc[:, 0:1])
            act(rstd[:, tb:tb + 1], rstd[:, tb:tb + 1], AF.Exp, [res("rstd", tb)], [res("rstd", tb)], scale=-0.5)

    def norm_to_hT(gi, hook=False, scr=None):
        if scr is None:
            scr = dict(junk=junk, junk_res=[res("junk")], gbuf=gbuf, gbuf_res=[res("gbuf")],
                       hn=hnb, hn_res=[[res("hn", 0)], [res("hn", 1)]])
        if gi != 0:
            dma("sp", scr["gbuf"], g_d[gi], [], scr["gbuf_res"], key="gbuf" if scr["gbuf"] is gbuf else "gbuf2")

        def rest_a(tb):
            hn = scr["hn"][tb % 2]
            dve("scalar_tensor_tensor", [res("x", tb), res("rstd", tb)] + scr["gbuf_res"], scr["hn_res"][tb % 2],
                hn, x_sb[:, tb, :], rstd[:, tb:tb + 1], scr["gbuf"], ALU.mult, ALU.mult)

        def rest_b(tb):
            hn = scr["hn"][tb % 2]
            for half in range(2):
                b = (tb % 2) * 2 + half
                pb = ps[b][:].bitcast(BF16)
                for kk in range(4):
                    k = half * 4 + kk
                    S.add("pe", lambda e, o_=pb[:, kk * 128:(kk + 1) * 128], i_=hn[:, k * 128:(k + 1) * 128]:
                          e.transpose(o_, i_, identb[:]), reads=scr["hn_res"][tb % 2] + [res("identb")],
                          writes=[res("ps", b)])
                src = pb[:, 0:512].rearrange("p (a b) -> p a b", a=4)
                dst = hT[:, half * 4:(half + 1) * 4, tb * 128:(tb + 1) * 128]
                if half == 0:
                    act(dst, src, AF.Copy, [res("ps", b)], [res("hT", tb, half)])
                else:
                    dve("tensor_copy", [res("ps", b)], [res("hT", tb, half)], dst, src)

        def step(tb):
            if tb < NTB:
                rms_stats([tb], scr["junk"], scr["junk_res"])
            if 2 <= tb < NTB + 2:
                rest_a(tb - 2)
            if tb >= 3:
                rest_b(tb - 3)
            if tb == NTB - 1:
                step(NTB)
                step(NTB + 1)
                step(NTB + 2)
        if hook:
            return step
        for tb in range(NTB):
            step(tb)

    def hT_reads(k, tt):
        return [res("hT", tb, k // 4) for tb in range(tt * 4, tt * 4 + 4)]

    def ffn_prefetch(fi):
        wgu_name = ("wgu1", "wgu2")[fi]
        pre = {}
        for g0 in range(0, 6, 2):
            s = next_slot()
            pre[g0] = s
            load_slot(s, [(0, 256, wgu_name, g0 * 128), (256, 256, wgu_name, DFF + g0 * 128)])
        return pre

    def ffn(fi, post_tb=None, cvt=None, pre=None):
        wgu_name = ("wgu1", "wgu2")[fi]
        wdn = wdn_d[fi]
        passes = [(0, 6), (6, 12), (12, 18), (18, 22)]
        pair = [0]
        obank = [0]
        for (c0, c1) in passes:
            nchp = c1 - c0
            gslots = {}
            for g0 in range(c0, c1, 2):
                if pre is not None and g0 in pre:
                    gslots[g0] = pre[g0]
                    continue
                s = next_slot()
                gslots[g0] = s
                load_slot(s, [(0, 256, wgu_name, g0 * 128),
                              (256, 256, wgu_name, DFF + g0 * 128)])
            dma("pool", wdn_sb[:, 0:nchp, :], wdn[c0 * 128:c1 * 128, :].rearrange("(j p) c -> p j c", p=128), [],
                [res("wdn")], key="wdn")
            if cvt:
                for _ in range(6):
                    if cvt:
                        convert_block(cvt.pop(0))
            for g0 in range(c0, c1, 2):
                s = gslots[g0]
                for jj in range(2):
                    j = g0 + jj
                    for tt in range(NTT):
                        pr = pair[0] % 2
                        pair[0] += 1
                        bg, bu = ps[pr * 2], ps[pr * 2 + 1]
                        for k in range(8):
                            mm(bg[:], slots[s][:, k, jj * 128:(jj + 1) * 128], hT[:, k, tt * 512:(tt + 1) * 512],
                               k == 0, k == 7, [res("slot", s, 0)] + hT_reads(k, tt), [res("ps", pr * 2)])
                        for k in range(8):
                            mm(bu[:], slots[s][:, k, 256 + jj * 128:256 + (jj + 1) * 128],
                               hT[:, k, tt * 512:(tt + 1) * 512],
                               k == 0, k == 7, [res("slot", s, 1)] + hT_reads(k, tt), [res("ps", pr * 2 + 1)])
                        act(sact[pr], bg[:], AF.Silu, [res("ps", pr * 2)], [res("sact", pr)])
                        dve("tensor_tensor", [res("sact", pr), res("ps", pr * 2 + 1)], [res("actT", j - c0, tt)],
                            actT[:, j - c0, tt * 512:(tt + 1) * 512], sact[pr], bu[:], ALU.mult)
            for tb in range(NTB):
                for fh in range(2):
                    b = 4 + obank[0] % 4
                    obank[0] += 1
                    for jj in range(nchp):
                        mm(ps[b][:], actT[:, jj, tb * 128:(tb + 1) * 128], wdn_sb[:, jj, fh * 512:(fh + 1) * 512],
                           jj == 0, jj == nchp - 1, [res("actT", jj, tb // 4), res("wdn")], [res("ps", b)])
                    xs = x_sb[:, tb, fh * 512:(fh + 1) * 512]
                    dve("scalar_tensor_tensor", [res("ps", b), res("x", tb)], [res("x", tb)],
                        xs, ps[b][:], 0.5, xs, ALU.mult, ALU.add)
                if post_tb is not None and c1 == NCH:
                    post_tb(tb)

    def proj_fm(bank, s, part, col0, tt, ktok=None):
        for k in range(8):
            mm(ps[bank][:], slots[s][:, k, col0:col0 + 128], hT[:, k, tt * 512:(tt + 1) * 512],
               k == 0, k == 7, [res("slot", s, part)] + hT_reads(k, tt), [res("ps", bank)])

    useq = [0]
    eseq = [0]
    trseq = [0]

    def gla_pipe(pre, mid, stepfill=None):
        state_only = False

        def glr_proj(tt):
            for k in range(8):
                mm(ps[2][0:16, :], wglr[:, k, :], hT[:, k, tt * 512:(tt + 1) * 512], k == 0, k == 7,
                   [res("wglr")] + hT_reads(k, tt), [res("ps", 2)])
            act(glrT[0:16, :], ps[2][0:16, :], AF.Copy, [res("ps", 2)], [res("glrT")])

        units = []
        pro = {}
        for tt in range(NTT):
            for hp in range(2):
                for tbi in range(4):
                    u = dict(tt=tt, hp=hp, tbi=tbi, tb=tt * 4 + tbi, n=useq[0], set=hp % 2)
                    useq[0] += 1
                    units.append(u)

        def prologue(tt, hp):
            st_ = hp % 2
            sqk = next_slot()
            load_slot(sqk, [(0, 256, "w_in", O_Q + hp * 256),
                            (256, 256, "w_in", O_K + hp * 256)])
            sv = next_slot()
            load_slot(sv, [(0, 512, "w_in", O_V + hp * 512)])
            pro[(tt, hp)] = sv
            pb = [0]

            def bank():
                b = (2, 5)[pb[0] % 2]
                pb[0] += 1
                return b
            for hh in range(2):
                if not state_only:
                    b = bank()
                    proj_fm(b, sqk, 0, hh * 128, tt)
                    act(qT[st_][hh], ps[b][:], AF.Copy, [res("ps", b)], [res("qT", st_, hh)], scale=float(128 ** -0.5))
                b = bank()
                proj_fm(b, sqk, 1, 256 + hh * 128, tt)
                act(kT[st_][hh], ps[b][:], AF.Copy, [res("ps", b)], [res("kT", st_, hh)])

        def prologue_r(tt, hp):
            srr = next_slot()
            load_slot(srr, [(0, 512, "w_in", O_R + hp * 512)])
            for hh in range(2):
                for vh in range(2):
                    b = (2, 5)[vh]
                    proj_fm(b, srr, 0, hh * 256 + vh * 128, tt)
                    act(sr[hh][vh], ps[b][:], AF.Silu, [res("ps", b)], [res("sr", hh, vh)])
                    dve("tensor_scalar", [res("sr", hh, vh), res("glag")], [res("sr", hh, vh)],
                        sr[hh][vh], sr[hh][vh], glag[:, (hp * 2 + hh) * 2 + vh:(hp * 2 + hh) * 2 + vh + 1], None, ALU.mult)

        def s0(u):
            hp, tb, n = u["hp"], u["tb"], u["n"]
            tok = slice(tb * 128, (tb + 1) * 128)
            ltok = slice(u["tbi"] * 128, (u["tbi"] + 1) * 128)
            sv = pro[(u["tt"], hp)]
            rv, r2 = n % NV, n % 2
            for k in range(8):
                mm(ps[2][:], hT[:, k, tok], slots[sv][:, k, :], k == 0, k == 7,
                   [res("slot", sv, 0), res("hT", tb, k // 4)], [res("ps", 2)])
            act(v_sb[rv], ps[2][:], AF.Copy, [res("ps", 2)], [res("v_sb", rv)])
            mm(ps[3][:, 0:256], glrT[0:17, ltok], gwu[0:17, hp * 256:(hp + 1) * 256], True, True,
               [res("glrT"), res("gwu")], [res("ps", 3)])
            act(e1[r2], ps[3][:, 0:256], AF.Exp, [res("ps", 3)], [res("e1", r2)], scale=-1.0)
            act(spb[r2], e1[r2], AF.Ln, [res("e1", r2), res("epsc")], [res("spb", r2)], bias=epsc[:, 1:2])

        def s1(u):
            n = u["n"]
            r2 = n % 2
            u["E"] = []
            for hh in range(2):
                re_ = eseq[0] % NE
                eseq[0] += 1
                u["E"].append(re_)
                if False:
                    pass
                else:
                    cb_ = 4 if hh == 0 else 1
                    mm(ps[cb_][:], spb[r2][:, hh * 128:(hh + 1) * 128], cmat[:], True, True,
                       [res("spb", r2), res("cmat")], [res("ps", cb_)])
                    act(Eb[re_], ps[cb_][:], AF.Exp, [res("ps", cb_)], [res("Eb", re_)])

        def s1_dve(u):
            n = u["n"]
            for hh in range(2):
                re_ = u["E"][hh]
                dve("tensor_copy", [res("Eb", re_)], [res("dec", n % NDEC, hh)], decb[n % NDEC][:, hh, :],
                    Eb[re_][:, 63:128:64])

        def s2(u):
            n, st_ = u["n"], u["set"]
            ltok = slice(u["tbi"] * 128, (u["tbi"] + 1) * 128)
            rq, rk = n % NQI, n % NTR
            for hh in range(2):
                re_ = u["E"][hh]
                E = Eb[re_]
                dve("tensor_tensor", [res("qT", st_, hh), res("Eb", re_)], [res("qq", rq, hh)],
                    qqb[rq][:, hh], E[:, 0:256].rearrange("p (a t) -> p a t", a=2),
                    qT[st_][hh][:, ltok].unsqueeze(1).broadcast_to([128, 2, 128]), ALU.mult)
                dve("tensor_tensor", [res("kT", st_, hh), res("Eb", re_)], [res("kk", rk, hh)],
                    kkb[rk][:, hh], E[:, 256:512].rearrange("p (a t) -> p a t", a=2),
                    kT[st_][hh][:, ltok].unsqueeze(1).broadcast_to([128, 2, 128]), ALU.mult)

        def s3(u):
            n = u["n"]
            rs_, rq, rk = n % NST, n % NQI, n % NTR
            p5b = ps[5][:].bitcast(BF16)
            for hh in range(2):
                mm(ps[5][:, hh * 128:(hh + 1) * 128], kkb[rk][:, hh, 0, :], qqb[rq][:, hh, 1, :], True, True,
                   [res("kk", rk, hh), res("qq", rq, hh)], [res("ps", 5)])
                S.add("pe", lambda e, o_=p5b[:, 512 + hh * 128:512 + (hh + 1) * 128], i_=kkb[rk][:, hh, 1, :]:
                      e.transpose(o_, i_, identb[:]), reads=[res("kk", rk, hh), res("identb")], writes=[res("ps", 5)])
            dve("tensor_tensor", [res("ps", 5), res("maskT")], [res("sTb", rs_)],
                sTb[rs_], ps[5][:, 0:256].rearrange("p (h t) -> p h t", h=2),
                maskT[:].unsqueeze(1).broadcast_to([128, 2, 128]), ALU.mult)
            act(ksT[rs_], p5b[:, 512:768].rearrange("p (h t) -> p h t", h=2), AF.Copy, [res("ps", 5)],
                [res("ksT", rs_)])

        def s4(u):
            hp, n, st_ = u["hp"], u["n"], u["set"]
            rv, rs_ = n % NV, n % NST
            ob = 7 if n % 2 == 0 else 0
            mm(ps[ob][:], zerob[:], qT[st_][0], True, False, [res("zerob"), res("qT", st_, 0)], [res("ps", ob)])
            for hh in range(2):
                for vh in range(2):
                    oc = (hh * 2 + vh) * 128
                    mm(ps[ob][:, oc:oc + 128], v_sb[rv][:, hh * 256 + vh * 128:hh * 256 + (vh + 1) * 128],
                       sTb[rs_][:, hh, :], False, False, [res("v_sb", rv), res("sTb", rs_)], [res("ps", ob)])
            s4c(u, 0)

        def s4b(u):
            s4c(u, 1)
            r2 = u["n"] % 2
            ob = 7 if u["n"] % 2 == 0 else 0
            act(o_sb[r2], ps[ob][:], AF.Copy, [res("ps", ob)], [res("o_sb", r2)])
            act(sq, ps[ob][:], AF.Square, [res("ps", ob)], [res("sq")])

        def s4c(u, c):
            hp, n = u["hp"], u["n"]
            ob = 7 if n % 2 == 0 else 0
            rv, rs_, rq, rd = n % NV, n % NST, n % NQI, n % NDEC
            cs = slice(c * 64, (c + 1) * 64)
            for hh in range(2):
                h = hp * 2 + hh
                for vh in range(2):
                    oc = (hh * 2 + vh) * 128 + c * 64
                    mm(ps[ob][:, oc:oc + 64], Sbf[:, h, vh * 128:(vh + 1) * 128], qqb[rq][:, hh, 0, cs],
                       False, c == 1 and hh == 1 and vh == 1, [res("Sbf", h), res("qq", rq, hh)],
                       [res("ps", ob)])
            for hh in range(2):
                mm(ps[6][:, hh * 256:(hh + 1) * 256], ksT[rs_][cs, hh, :], v_sb[rv][cs, hh * 256:(hh + 1) * 256],
                   True, True, [res("ksT", rs_), res("v_sb", rv)], [res("ps", 6)])
            for hh in range(2):
                h = hp * 2 + hh
                dve("scalar_tensor_tensor", [res("Sst", h), res("dec", rd, hh), res("ps", 6)], [res("Sbf", h)],
                    Sbf[:, h, :], Sst[:, h, :], decb[rd][:, hh, c:c + 1], ps[6][:, hh * 256:(hh + 1) * 256],
                    ALU.mult, ALU.add)
            for hh in range(2):
                h = hp * 2 + hh
                dve("scalar_tensor_tensor", [res("Sst", h), res("dec", rd, hh), res("ps", 6)], [res("Sst", h)],
                    Sst[:, h, :], Sst[:, h, :], decb[rd][:, hh, c:c + 1], ps[6][:, hh * 256:(hh + 1) * 256],
                    ALU.mult, ALU.add)

        def s5(u):
            hp, n = u["hp"], u["n"]
            r2 = n % 2
            ltok = slice(u["tbi"] * 128, (u["tbi"] + 1) * 128)
            for hh in range(2):
                for vh in range(2):
                    oc = (hh * 2 + vh) * 128
                    mm(ps[3][:, 256 + hh * 128:256 + (hh + 1) * 128], onesb[:], sq[:, oc:oc + 128], vh == 0, vh == 1,
                       [res("onesb"), res("sq")], [res("ps", 3)])
            act(rs, ps[3][:, 256:512], AF.Ln, [res("ps", 3), res("epsc")], [res("rs")], bias=epsc[:, 0:1])
            act(rs, rs, AF.Exp, [res("rs")], [res("rs")], scale=-0.5)

        def s5_dve(u):
            hp, n = u["hp"], u["n"]
            r2 = n % 2
            ltok = slice(u["tbi"] * 128, (u["tbi"] + 1) * 128)
            ov = o_sb[r2].rearrange("p (h v t) -> p h v t", h=2, v=2)
            dve("tensor_tensor", [res("o_sb", r2), res("rs")], [res("o_sb", r2)],
                ov, ov, rs.rearrange("p (h t) -> p h t", h=2).unsqueeze(2).broadcast_to([128, 2, 2, 128]), ALU.mult)
            dve("tensor_tensor", [res("o_sb", r2)] + [res("sr", a // 2, a % 2) for a in range(4)],
                [res("BUF1", hp * 4 + a) for a in range(4)],
                BUF1[:, hp * 4:(hp + 1) * 4, ltok], o_sb[r2].rearrange("p (a t) -> p a t", a=4),
                srall3[:, :, ltok], ALU.mult)

        order = [(s4, 4), (s3, 3), (s5, 5), (s2, 2), (s1, 1), (s4b, 4), (s1_dve, 1), (s5_dve, 5)]
        nst = 6
        nu = len(units)
        pre()
        for step in range(nu + nst - 1):
            for fn_, lag in order:
                ui = step - lag
                if 0 <= ui < nu:
                    fn_(units[ui])
            if step >= 12 and (step - 12) % 8 == 0:
                mid((step - 12) // 8)
            if step < nu:
                u = units[step]
                if u["hp"] == 0 and u["tbi"] == 0:
                    glr_proj(u["tt"])
                if u["tbi"] == 0:
                    prologue(u["tt"], u["hp"])
                s0(u)
            if step - 4 >= 0 and step - 4 < nu and units[step - 4]["tbi"] == 0:
                prologue_r(units[step - 4]["tt"], units[step - 4]["hp"])
            if stepfill is not None:
                stepfill(step)

    def gla_state_tile(tt):
        for k in range(8):
            mm(ps[0][0:16, :], wglr[:, k, :], hT[:, k, tt * 512:(tt + 1) * 512], k == 0, k == 7,
               [res("wglr")] + hT_reads(k, tt), [res("ps", 0)])
        act(glrT[0:16, :], ps[0][0:16, :], AF.Copy, [res("ps", 0)], [res("glrT")])
        spq = [e1[0], e1[1], spb[0], spb[1]]
        spr = [("e1", 0), ("e1", 1), ("spb", 0), ("spb", 1)]
        for hp in range(2):
            st_ = hp % 2
            sqk = next_slot()
            load_slot(sqk, [(0, 256, "w_in", O_Q + hp * 256), (256, 256, "w_in", O_K + hp * 256)])
            sv = next_slot()
            load_slot(sv, [(0, 512, "w_in", O_V + hp * 512)])
            for hh in range(2):
                proj_fm(hh, sqk, 1, 256 + hh * 128, tt)
                act(kT[st_][hh], ps[hh][:], AF.Copy, [res("ps", hh)], [res("kT", st_, hh)])
            for tbi in range(4):
                tb = tt * 4 + tbi
                tok = slice(tb * 128, (tb + 1) * 128)
                ltok = slice(tbi * 128, (tbi + 1) * 128)
                vb = 2 if tbi % 2 == 0 else 7
                for k in range(8):
                    mm(ps[vb][:], hT[:, k, tok], slots[sv][:, k, :], k == 0, k == 7,
                       [res("slot", sv, 0), res("hT", tb, k // 4)], [res("ps", vb)])
                act(v_sb[tbi], ps[vb][:], AF.Copy, [res("ps", vb)], [res("v_sb", tbi)])
                mm(ps[3][:, 0:256], glrT[0:17, ltok], gwu[0:17, hp * 256:(hp + 1) * 256], True, True,
                   [res("glrT"), res("gwu")], [res("ps", 3)])
                act(Eb[0][:, 0:256], ps[3][:, 0:256], AF.Exp, [res("ps", 3)], [res("Eb", 0)], scale=-1.0)
                act(spq[tbi], Eb[0][:, 0:256], AF.Ln, [res("Eb", 0), res("epsc")], [res(*spr[tbi])],
                    bias=epsc[:, 1:2])
            SB, TB_, KB = [4, 1], [5, 0], [6, 3]
            for hh in range(2):
                hc = slice(hh * 128, (hh + 1) * 128)
                for j in range(4):
                    for i in range(3, j - 1, -1):
                        mm(ps[SB[hh]][:, j * 128:(j + 1) * 128], spq[i][:, hc], lstr[:] if i == j else onesf[:],
                           i == 3, i == j, [res(*spr[i]), res("lstr"), res("onesf")], [res("ps", SB[hh])])
                for i in range(4):
                    mm(ps[TB_[hh]][:, 0:2], spq[i][:, hc], onesf[:, 0:2], i == 0, i == 3,
                       [res(*spr[i]), res("onesf")], [res("ps", TB_[hh])])
            for hh in range(2):
                re_ = 1 + hh
                act(Eb[re_], ps[SB[hh]][:], AF.Exp, [res("ps", SB[hh])], [res("Eb", re_)], scale=-16.0)
                act(decb[hh][:, 0, :], ps[TB_[hh]][:, 0:2], AF.Exp, [res("ps", TB_[hh])], [res("dec", hh, 0)],
                    scale=-16.0)
            for hh in range(2):
                re_ = 1 + hh
                dve("tensor_tensor", [res("kT", st_, hh), res("Eb", re_)], [res("qT", st_, hh)],
                    qT[st_][hh], kT[st_][hh], Eb[re_], ALU.mult)
            for hh in range(2):
                ksb = qT[st_][hh]
                pbb = ps[TB_[hh]][:].bitcast(BF16)
                for tbi in range(4):
                    S.add("pe", lambda e, o_=pbb[:, 512 + tbi * 128:512 + (tbi + 1) * 128],
                          i_=ksb[:, tbi * 128:(tbi + 1) * 128]: e.transpose(o_, i_, identb[:]),
                          reads=[res("qT", st_, hh), res("identb")], writes=[res("ps", TB_[hh])])
            for hh in range(2):
                pbb = ps[TB_[hh]][:].bitcast(BF16)
                act(sr[hh][0], pbb[:, 512:1024], AF.Copy, [res("ps", TB_[hh])], [res("sr", hh, 0)])
            for hh in range(2):
                kst = sr[hh][0]
                for tbi in range(4):
                    mm(ps[KB[hh]][:, 0:256], kst[:, tbi * 128:(tbi + 1) * 128],
                       v_sb[tbi][:, hh * 256:(hh + 1) * 256], tbi == 0, tbi == 3,
                       [res("sr", hh, 0), res("v_sb", tbi)], [res("ps", KB[hh])])
            for hh in range(2):
                h = hp * 2 + hh
                dve("scalar_tensor_tensor", [res("Sst", h), res("dec", hh, 0), res("ps", KB[hh])], [res("Sst", h)],
                    Sst[:, h, :], Sst[:, h, :], decb[hh][:, 0, 0:1], ps[KB[hh]][:, 0:256],
                    ALU.mult, ALU.add)

    conv_pre = {}

    def conv_tile(tt):
        pc = [0]
        for cg in range(2):
            sA = []
            for pq in range(2):
                c0 = cg * 4 + pq * 2
                if (tt, cg, pq) in conv_pre:
                    s = conv_pre.pop((tt, cg, pq))
                else:
                    s = next_slot()
                    load_slot(s, [(0, 256, "w_in", O_CC + c0 * 128),
                                  (256, 256, "w_in", O_CU + c0 * 128)])
                for cc_ in range(2):
                    c = c0 + cc_
                    r2 = pc[0] % 2
                    pc[0] += 1
                    proj_fm(0 + r2 * 2, s, 0, cc_ * 128, tt)
                    proj_fm(1 + r2 * 2, s, 1, 256 + cc_ * 128, tt)
                    act(ccs[0], ps[0 + r2 * 2][:], AF.Copy, [res("ps", 0 + r2 * 2)], [res("ccs", 0)])
                    pb = pbuf[r2]
                    dve("tensor_copy", [res("phalo", c)], [res("pbuf", r2)], pb[:, 0:2], phalo[:, c, :])
                    dve("tensor_tensor", [res("ccs", 0), res("ps", 1 + r2 * 2)], [res("pbuf", r2)],
                        pb[:, 2:514], ccs[0], ps[1 + r2 * 2][:], ALU.mult)
                    dve("tensor_copy", [res("pbuf", r2)], [res("phalo", c)], phalo[:, c, :], pb[:, 512:514])
                    act(tsc[r2], pb[:, 2:514], AF.Copy, [res("pbuf", r2), res("convw")], [res("tsc", r2)],
                        scale=convw[:, c, 2:3])
                    dve("scalar_tensor_tensor", [res("pbuf", r2), res("tsc", r2), res("convw")], [res("tsc", r2)],
                        tsc[r2], pb[:, 1:513], convw[:, c, 1:2], tsc[r2], ALU.mult, ALU.add)
                    dve("scalar_tensor_tensor", [res("pbuf", r2), res("tsc", r2), res("convw")], [res("tsc", r2)],
                        tsc[r2], pb[:, 0:512], convw[:, c, 0:1], tsc[r2], ALU.mult, ALU.add)
                    act(BUF1[:, c, :], tsc[r2], AF.Copy, [res("tsc", r2)], [res("BUF1", c)])
            s = next_slot()
            load_slot(s, [(0, 512, "w_in", O_CB + cg * 512)])
            for cc_ in range(4):
                c = cg * 4 + cc_
                b = 4 + cc_ % 2
                proj_fm(b, s, 0, cc_ * 128, tt)
                dve("tensor_tensor", [res("ps", b), res("BUF1", c)], [res("BUF1", c)],
                    BUF1[:, c, :], ps[b][:], BUF1[:, c, :], ALU.mult)

    def outproj_gate(tt, w_name, gate_off, accumulate):
        cnt = 0
        for mg in range(4):
            sx = next_slot()
            load_slot(sx, [(0, 256, w_name, mg * 256), (256, 256, "w_in", gate_off + mg * 256)])
            sy = sx
            for mm_ in range(2):
                m = mg * 2 + mm_
                r2 = cnt % 2
                cnt += 1
                by, bg_ = r2 * 2, r2 * 2 + 1
                for k in range(8):
                    mm(ps[by][:], slots[sx][:, k, mm_ * 128:(mm_ + 1) * 128], BUF1[:, k, :], k == 0, k == 7,
                       [res("slot", sx, 0), res("BUF1", k)], [res("ps", by)])
                proj_fm(bg_, sy, 1, 256 + mm_ * 128, tt)
                act(sga[r2], ps[bg_][:], AF.Sigmoid, [res("ps", bg_)], [res("sga", r2)])
                if not accumulate:
                    dve("tensor_tensor", [res("sga", r2), res("ps", by)], [res("BUF2", m)],
                        BUF2[:, m, :], sga[r2], ps[by][:], ALU.mult)
                else:
                    dve("tensor_tensor", [res("sga", r2), res("ps", by)], [res("sga", r2)],
                        sga[r2], sga[r2], ps[by][:], ALU.mult)
                    dve("tensor_tensor", [res("sga", r2), res("BUF2", m)], [res("BUF2", m)],
                        BUF2[:, m, :], sga[r2], BUF2[:, m, :], ALU.add)

    def wo_tile(tt, post_tb=None):
        cnt = 0
        for fh in range(2):
            s = next_slot()
            load_slot(s, [(0, 512, "w_o", fh * 512)])
            for tbi in range(4):
                tb = tt * 4 + tbi
                b = 4 + cnt % 4
                cnt += 1
                for k in range(8):
                    mm(ps[b][:], BUF2[:, k, tbi * 128:(tbi + 1) * 128], slots[s][:, k, :], k == 0, k == 7,
                       [res("BUF2", k), res("slot", s, 0)], [res("ps", b)])
                xs = x_sb[:, tb, fh * 512:(fh + 1) * 512]
                dve("tensor_tensor", [res("ps", b), res("x", tb)], [res("x", tb)], xs, ps[b][:], xs, ALU.add)
                if post_tb is not None and fh == 1:
                    post_tb(tb)

    halo_slots = []

    def mixer_prefetch():
        dma("pool", wglr[:], wglr_d.rearrange("(k p) c -> p k c", p=128), [], [res("wglr")],
            key="wglr")
        if EXCH:
            for cg in range(3):
                s = next_slot()
                c0 = cg * 2
                load_slot(s, [(0, 256, "w_in", O_CC + c0 * 128), (256, 256, "w_in", O_CU + c0 * 128)])
                halo_slots.append(s)

    def mixer():
        dve("memset", [], [res("glrT")], glrT[:, :], 1.0)
        RG = [[0, 1], [2, 3], [4, 5], [6, 7]]
        if EXCH:
            halo_send = tsc[0]
            for cg in range(4):
                c0 = cg * 2
                if cg < len(halo_slots):
                    s = halo_slots[cg]
                else:
                    s = next_slot()
                    load_slot(s, [(0, 256, "w_in", O_CC + c0 * 128),
                                  (256, 256, "w_in", O_CU + c0 * 128)])
                for cc_ in range(2):
                    c = c0 + cc_
                    for which in range(2):
                        for k in range(8):
                            mm(ps[which][:, 0:2], slots[s][:, k, which * 256 + cc_ * 128:which * 256 + (cc_ + 1) * 128],
                               hT[:, k, T - 2:T], k == 0, k == 7, [res("slot", s, which), res("hT", 15, k // 4)],
                               [res("ps", which)])
                    act(ccs[0][:, 0:2], ps[0][:, 0:2], AF.Copy, [res("ps", 0)], [res("ccs", 0)])
                    dve("tensor_tensor", [res("ccs", 0), res("ps", 1)], [res("halo_send")],
                        halo_send[:, c * 2:c * 2 + 2], ccs[0][:, 0:2], ps[1][:, 0:2], ALU.mult)
            dma("sp", hsend_d, halo_send[:, 0:16], [res("halo_send")], [res("hsend")], key="hsend")
            S.add("pool", lambda e: e.collective_compute(
                "AllGather", ALU.bypass, replica_groups=RG, ins=[hsend_d], outs=[hrecv_d]),
                reads=[res("hsend")], writes=[res("hrecv")], dma=True, key="cch", inc=CC_INC[0])
            dma("sp", halo_in[:].rearrange("p c t -> p (c t)"), hrecv_d[0:128, :], [res("hrecv")],
                [res("halo_in")], key="xr1")
            for tt in range(NTT):
                gla_state_tile(tt)
            dve("tensor_scalar", [res("halo_in"), res("flag")] + [res("phalo", c) for c in range(8)],
                [res("phalo", c) for c in range(8)],
                phalo[:].rearrange("p c t -> p (c t)"), halo_in[:].rearrange("p c t -> p (c t)"), flag[:, 0:1], None,
                ALU.mult)
            s_ = next_slot()
            load_slot(s_, [(0, 256, "w_in", O_CC), (256, 256, "w_in", O_CU)])
            conv_pre[(0, 0, 0)] = s_
            dma("sp", xsend_d, Sst[:].rearrange("p h v -> p (h v)"), [res("Sst", h) for h in range(4)],
                [res("xsend")], key="xsend0")
            S.add("pool", lambda e: e.collective_compute(
                "AllGather", ALU.bypass, replica_groups=RG, ins=[xsend_d], outs=[xrecv_d]),
                reads=[res("xsend")], writes=[res("xrecv")], dma=True, key="cc", inc=CC_INC[0])
            dma("sp", Sst[:].rearrange("p h v -> p (h v)"), xrecv_d[0:128, :],
                [res("xrecv")], [res("Sst", h) for h in range(4)], key="xr0")
        def pre():
            conv_tile(0)
            outproj_gate(0, "w_out_conv", O_GA, False)
            if EXCH:
                for h in range(4):
                    dve("tensor_scalar", [res("flag"), res("Sst", h)], [res("Sst", h)],
                        Sst[:, h, :], Sst[:, h, :], flag[:, 0:1], None, ALU.mult)
                    act(Sbf[:, h, :], Sst[:, h, :], AF.Copy, [res("Sst", h)], [res("Sbf", h)])

        n2 = {}

        def mid(tt):
            outproj_gate(tt, "w_out_gla", O_GB, True)
            wo_tile(tt, n2.get("step") if tt == NTT - 1 else None)
            if tt + 1 < NTT:
                conv_tile(tt + 1)
                outproj_gate(tt + 1, "w_out_conv", O_GA, False)
            if tt == NTT - 2 and STAGE >= 3:
                scr = dict(junk=ccs[0].bitcast(BF16), junk_res=[res("ccs", 0)],
                           gbuf=arena[:, TSC_OFF:TSC_OFF + 2048].bitcast(F32), gbuf_res=[res("tsc", 0), res("tsc", 1)],
                           hn=[pbuf[0].bitcast(BF16)[:, 0:1024], pbuf[1].bitcast(BF16)[:, 0:1024]],
                           hn_res=[[res("pbuf", 0)], [res("pbuf", 1)]])
                n2["step"] = norm_to_hT(2, hook=True, scr=scr)

        def stepfill(step):
            if "step" in n2 and 29 <= step <= 34:
                for tb in ((step - 29) * 2, (step - 29) * 2 + 1):
                    n2["step"](tb)

        gla_pipe(pre, mid, stepfill)

    def final_out(do_norm, hook=False):
        if do_norm:
            load_g(3)

        def fin(tb):
            st = ostage[tb % 2]
            if do_norm:
                act(st, x_sb[:, tb, :], AF.Copy, [res("x", tb), res("rstd", tb)],
                    [res("ostage", tb % 2), res("ostageR", tb % 2)], scale=rstd[:, tb:tb + 1])
                S.add("pool", lambda e, st=st: e.tensor_tensor(st[:, 0:512], st[:, 0:512], gbuf[:, 0:512], ALU.mult),
                      reads=[res("ostage", tb % 2), res("gbuf")], writes=[res("ostage", tb % 2)])
                dve("tensor_tensor", [res("ostageR", tb % 2), res("gbuf")], [res("ostageR", tb % 2)],
                    st[:, 512:1024], st[:, 512:1024], gbuf[:, 512:1024], ALU.mult)
            else:
                dve("tensor_copy", [res("x", tb)], [res("ostage", tb % 2)], st, x_sb[:, tb, :])
            dma("sp", out_d[tb * 128:(tb + 1) * 128, :], st, [res("ostage", tb % 2), res("ostageR", tb % 2)],
                [res("outd", tb % 2)], key=("out", tb % 2))

        def step(tb):
            if do_norm:
                rms_stats([tb])
            if tb >= 1:
                fin(tb - 1)
            if tb == NTB - 1:
                fin(tb)
        if hook:
            return step
        for tb in range(NTB):
            step(tb)

    norm_to_hT(0)
    if STAGE >= 2:
        ffn(0, norm_to_hT(1, hook=True), cvt=mixer_keys())
    else:
        ffn(0)
    if STAGE >= 2:
        mixer_prefetch()
        S.barrier()
        mixer()
    if STAGE >= 3:
        pre2 = ffn_prefetch(1)
        S.barrier()
        ffn(1, final_out(STAGE >= 4, hook=True), pre=pre2)
    else:
        final_out(STAGE >= 4)

    keys = S.finalize()
    sem_cms = {}
    for e in ENGS:
        cm = nc.semaphore(f"sem_{e}")
        sem_cms[e] = cm.__enter__()
        es.append(cm)
    for i, k in enumerate(keys):
        cm = nc.semaphore(f"semk_{i}")
        sem_cms[("k", k)] = cm.__enter__()
        es.append(cm)

    def sem_of(op):
        return sem_cms[("k", op.key)] if op.dma else sem_cms[op.eng]

    def emit(eng_name, e):
        waited = {}
        for op in S.streams[eng_name]:
            need = {}
            for d in op.deps:
                sm = sem_of(d)
                if need.get(sm, (0,))[0] < d.val:
                    need[sm] = (d.val, sm)
            for sm, (val, _) in need.items():
                if waited.get(sm, 0) < val:
                    e.wait_ge(sm, val)
                    waited[sm] = val
            ins = op.fn(e)
            if op.sig:
                ins.then_inc(sem_of(op), op.inc if op.dma else 1)
        if eng_name == "sp":
            for k in keys:
                if isinstance(k, tuple) and k[0] == "out":
                    tot = sum(o_.inc for o_ in S.streams["sp"] if o_.dma and o_.key == k)
                    e.wait_ge(sem_cms[("k", k)], tot)

    with nc.Block() as block:
        @block.tensor
        def _(e):
            emit("pe", e)

        @block.scalar
        def _(e):
            emit("act", e)

        @block.vector
        def _(e):
            emit("dve", e)

        @block.gpsimd
        def _(e):
            emit("pool", e)

        @block.sync
        def _(e):
            emit("sp", e)

    for cm in reversed(es):
        cm.__exit__(None, None, None)
    return nc


CC_INC = [1]


def _consts():
    s = np.arange(128)[:, None]
    t = np.arange(128)[None, :]
    same = (s // 64) == (t // 64)
    tri = (same & (s <= t)).astype(np.float64)
    ref = (t // 64) * 64 + 31
    last = (t // 64) * 64 + 63
    tri_ref = (same & (s <= ref)).astype(np.float64)
    tri_last = (same & (s <= last)).astype(np.float64)
    Dm = tri - tri_ref
    Lm = tri_last - tri
    cm = np.concatenate([tri, Dm, -Dm, Lm], axis=1) * (-1.0 / 16.0)
    cm1 = (s > t).astype(np.float64) / 256.0
    maskT = (same & (s <= t)).astype(np.float32)
    return (np.eye(128, dtype=np.float32), cm.astype(np.float32), maskT,
            np.full((128, 128), 1.0 / 256.0, dtype=np.float32), cm1.astype(np.float32))


def kernel(x, norm_ffn1_g, ffn1_w_gu, ffn1_w_down, norm_mix_g, w_in, conv_w, gate_w_up, gate_b,
           gla_norm_g, w_out_conv, w_out_gla, w_o, norm_ffn2_g, ffn2_w_gu, ffn2_w_down, final_norm_g):
    f = lambda a: np.ascontiguousarray(np.asarray(a, dtype=np.float32))
    x = f(x)
    ident, cmat, maskT, onesm, cmat1 = _consts()
    rep = lambda g: np.ascontiguousarray(np.broadcast_to(f(g).reshape(1, D), (128, D)))
    shared = {
        "g1": rep(norm_ffn1_g), "gm": rep(norm_mix_g), "g2": rep(norm_ffn2_g), "gf": rep(final_norm_g),
        "wdn1": f(ffn1_w_down).reshape(DFF, D), "wdn2": f(ffn2_w_down).reshape(DFF, D),
        "convw": np.ascontiguousarray(f(conv_w).reshape(3, 8, 128).transpose(2, 1, 0)),
        "glag": np.ascontiguousarray(f(gla_norm_g).reshape(8, 128).T),
        "gwu": np.ascontiguousarray(np.concatenate([f(gate_w_up).reshape(16, 512), f(gate_b).reshape(1, 512)], axis=0)),
        "ident": ident, "cmat": cmat, "lstrict": cmat1, "maskT": maskT, "onesm": onesm,
    }
    nc = build_nc()
    W = {"wgu1": f(ffn1_w_gu).reshape(D, 2 * DFF), "wgu2": f(ffn2_w_gu).reshape(D, 2 * DFF),
         "w_in": f(w_in).reshape(D, INC), "w_out_conv": f(w_out_conv).reshape(D, D),
         "w_out_gla": f(w_out_gla).reshape(D, D), "w_o": f(w_o).reshape(D, D)}
    assert len(SLOTKEYS) <= NSLOTS, len(SLOTKEYS)
    wslots = np.empty((NSLOTS, 128, 8, 512), dtype=np.float32)
    for key, bid in SLOTKEYS.items():
        for (co, n, name, c0) in key:
            wslots[bid][:, :, co:co + n] = W[name][:, c0:c0 + n].reshape(8, 128, n).transpose(1, 0, 2)
    shared["wslots"] = wslots.reshape(NSLOTS, 128, 4096)
    shared["wglr_in"] = np.ascontiguousarray(W["w_in"][:, O_G:O_G + 16])
    in_maps = []
    for c in range(8):
        b, half = c // 2, c % 2
        m = dict(shared)
        m["x"] = np.ascontiguousarray(x[b, half * T:(half + 1) * T, :])
        m["flag"] = np.full((128, 1), float(half), dtype=np.float32)
        in_maps.append(m)
    res = run_bass_kernel_spmd(nc, in_maps, core_ids=list(range(8)))
    out = np.empty((4, 2 * T, D), dtype=np.float32)
    for c in range(8):
        b, half = c // 2, c % 2
        out[b, half * T:(half + 1) * T, :] = np.asarray(res.results[c]["out"], dtype=np.float32)
    return out
```

```python
import os
import numpy as np
import concourse.bass as bass
import concourse.mybir as mybir
from concourse.bass_utils import run_bass_kernel_spmd

F32 = mybir.dt.float32
BF16 = mybir.dt.bfloat16
AF = mybir.ActivationFunctionType
ALU = mybir.AluOpType

D = 1024
DFF = 2816
NCH = 22
T = 2048
NTB = 16
NTT = 4
EPS = 1e-6
INC = 8208
O_CB, O_CC, O_CU, O_Q, O_K, O_V, O_G, O_R, O_GA, O_GB = 0, 1024, 2048, 3072, 3584, 4096, 5120, 5136, 6160, 7184

STAGE = int(os.environ.get("KSTAGE", "4"))
EXCH = int(os.environ.get("KEXCH", "1"))
DBG = int(os.environ.get("KDBG", "0"))

ENGS = ["pe", "act", "dve", "pool", "sp"]


class Res:
    __slots__ = ("name", "w", "r", "parent")

    def __init__(self, name):
        self.name = name
        self.w = None
        self.r = {}
        self.parent = None


class Op:
    __slots__ = ("eng", "fn", "deps", "sig", "dma", "key", "val", "inc")


class Sched:
    def __init__(self):
        self.streams = {e: [] for e in ENGS}
        self.barrier_deps = {}

    def add(self, eng, fn, reads=(), writes=(), dma=False, key=None, inc=16):
        op = Op()
        op.eng, op.fn, op.dma, op.key, op.sig, op.deps, op.val, op.inc = eng, fn, dma, key, dma, set(), 0, inc

        def consider(d, kind):
            if d is None or d is op:
                return
            if d.dma or dma:
                op.deps.add(d)
                return
            if d.eng == eng:
                if eng == "pe":
                    return
            op.deps.add(d)

        for r in reads:
            consider(r.w, "raw")
            if r.name.startswith("('ps'"):
                for d in r.r.values():
                    if d.eng != eng:
                        consider(d, "raw")
        for w in writes:
            consider(w.w, "waw")
            for d in w.r.values():
                consider(d, "war")
            if w.parent is not None:
                for d in w.parent.r.values():
                    consider(d, "war")
        if eng in self.barrier_deps:
            for d in self.barrier_deps.pop(eng):
                consider(d, "raw")
        rk = ("dma", key) if dma else eng
        for r in reads:
            r.r[rk] = op
            if r.parent is not None:
                r.parent.r[rk] = op
        for w in writes:
            w.w = op
            w.r = {}
        self.streams[eng].append(op)
        return op

    def barrier(self):
        last = [s[-1] for s in self.streams.values() if s]
        for e in ENGS:
            self.barrier_deps[e] = list(last)

    def finalize(self):
        for s in self.streams.values():
            for op in s:
                for d in op.deps:
                    d.sig = True
        keycnt = {}
        for e, s in self.streams.items():
            c = 0
            for op in s:
                if op.dma:
                    keycnt[op.key] = keycnt.get(op.key, 0) + op.inc
                    op.val = keycnt[op.key]
                elif op.sig:
                    c += 1
                    op.val = c
        return sorted(keycnt.keys(), key=str)


SLOTKEYS = {}
NSLOTS = 44


def build_nc():
    SLOTKEYS.clear()
    nc = bass.Bass("TRN2", target_bir_lowering=False)

    def din(name, shape, dt=F32):
        return nc.dram_tensor(name, list(shape), dt, kind="ExternalInput").ap()

    x_d = din("x", [T, D])
    out_d = nc.dram_tensor("out", [T, D], F32, kind="ExternalOutput").ap()
    g_d = [din(n, [128, D]) for n in ("g1", "gm", "g2", "gf")]
    wdn_d = [din("wdn1", [DFF, D]), din("wdn2", [DFF, D])]
    wglr_d = din("wglr_in", [D, 16])
    wslots_d = din("wslots", [NSLOTS, 128, 4096])
    wsb_d = nc.dram_tensor("wsb", [NSLOTS, 128, 4096], BF16).ap()
    convw_d = din("convw", [128, 8, 3])
    glag_d = din("glag", [128, 8])
    gwu_d = din("gwu", [17, 512])
    flag_d = din("flag", [128, 1])
    ident_d = din("ident", [128, 128])
    cmat_d = din("cmat", [128, 512])
    lst_d = din("lstrict", [128, 128])
    maskT_d = din("maskT", [128, 128])
    onesm_d = din("onesm", [128, 128])
    XW = 4 * 256 + 16
    if EXCH:
        xsend_d = nc.dram_tensor("xsend", [128, 1024], F32).ap()
        xrecv_d = nc.dram_tensor("xrecv", [256, 1024], F32).ap()
        hsend_d = nc.dram_tensor("hsend", [128, 16], F32).ap()
        hrecv_d = nc.dram_tensor("hrecv", [256, 16], F32).ap()

    S = Sched()
    es = []

    def sb(name, shape, dt):
        cm = nc.sbuf_tensor(name, list(shape), dt)
        t = cm.__enter__()
        es.append(cm)
        return t

    def pst(name):
        cm = nc.psum_tensor(name, [128, 512], F32)
        t = cm.__enter__()
        es.append(cm)
        return t

    x_sb = sb("x_sb", [128, NTB, D], F32)
    hT = sb("hT", [128, 8, T], BF16)
    ARN = 37184
    arena = sb("arena", [128, ARN], BF16)
    slots = [sb(f"slot{i}", [128, 8, 512], BF16) for i in range(3)]
    ss = sb("ss", [128, NTB], F32)
    rstd = sb("rstd", [128, NTB], F32)
    identf = sb("identf", [128, 128], F32)
    identb = sb("identb", [128, 128], BF16)
    cmat = sb("cmat_sb", [128, 512], F32)
    lstr = sb("lstr_sb", [128, 128], F32)
    maskT = sb("maskT_sb", [128, 128], F32)
    onesf = sb("onesf", [128, 128], F32)
    onesb = sb("onesb", [128, 128], BF16)
    convw = sb("convw_sb", [128, 8, 3], F32)
    glag = sb("glag_sb", [128, 8], F32)
    gwu = sb("gwu_sb", [32, 512], F32)
    flag = sb("flag_sb", [128, 1], F32)
    wglr = sb("wglr", [128, 8, 16], BF16)
    Sst = sb("Sst", [128, 4, 256], F32)
    Sbf = sb("Sbf", [128, 4, 256], BF16)
    zerob = sb("zerob", [128, 128], BF16)
    phalo = sb("phalo", [128, 8, 2], F32)
    halo_in = sb("halo_in", [128, 8, 2], F32)
    epsc = sb("epsc", [128, 2], F32)
    ps = [pst(f"ps{i}") for i in range(8)]

    def abf(off, n):
        return arena[:, off:off + n]

    def af32(off, n):
        return arena[:, off:off + 2 * n].bitcast(F32)

    actT = abf(0, 6 * T).rearrange("p (j t) -> p j t", j=6)
    wdn_sb = abf(6 * T, 6 * D).rearrange("p (j c) -> p j c", j=6)
    sact = [abf(6 * T + 6 * D + i * 512, 512) for i in range(2)]
    NOST = 4
    ostage = [af32(25600 + i * 2048, 1024) for i in range(NOST)]
    junk = abf(20480, 1024)
    gbuf = af32(21504, 1024)
    hnb = [abf(23552, 1024), abf(24576, 1024)]
    o = 0

    def take_bf(n):
        nonlocal o
        v = abf(o, n)
        o += n
        return v

    def take_f32(n):
        nonlocal o
        v = af32(o, n)
        o += 2 * n
        return v

    BUF1 = take_bf(8 * 512).rearrange("p (k t) -> p k t", k=8)
    BUF2 = take_bf(8 * 512).rearrange("p (k t) -> p k t", k=8)
    pbuf = [take_f32(514) for _ in range(2)]
    ccs = [take_f32(512) for _ in range(1)]
    sga = [take_f32(512) for _ in range(2)]
    TSC_OFF = o
    tsc = [take_f32(512) for _ in range(2)]
    qT = [[take_bf(512) for _ in range(2)] for _ in range(2)]
    kT = [[take_bf(512) for _ in range(2)] for _ in range(2)]
    srall = take_bf(2048)
    sr = [[srall[:, (hh * 2 + vh) * 512:(hh * 2 + vh + 1) * 512] for vh in range(2)] for hh in range(2)]
    srall3 = srall.rearrange("p (a t) -> p a t", a=4)
    NV, NE, NQI, NTR, NST, NDEC = 4, 3, 3, 2, 3, 6
    v_sb = [take_bf(512) for _ in range(NV)]
    e1 = [take_f32(256) for _ in range(2)]
    spb = [take_f32(256) for _ in range(2)]
    Eb = [take_f32(512) for _ in range(NE)]
    decb = [take_f32(4).rearrange("p (h c) -> p h c", h=2) for _ in range(NDEC)]
    qqb = [take_bf(512).rearrange("p (h a t) -> p h a t", h=2, a=2) for _ in range(NQI)]
    kkb = [take_bf(512).rearrange("p (h a t) -> p h a t", h=2, a=2) for _ in range(NTR)]
    sTb = [take_bf(256).rearrange("p (h t) -> p h t", h=2) for _ in range(NST)]
    ksT = [take_bf(256).rearrange("p (h t) -> p h t", h=2) for _ in range(NST)]
    o_sb = [take_f32(512) for _ in range(2)]
    sq = take_bf(512)
    rs = take_f32(256)
    glrT = arena[0:32, o:o + 1024].bitcast(F32)
    o += 1024
    assert o <= ARN, o

    R = {}

    def res(*k):
        if k not in R:
            R[k] = Res(str(k))
            if k[0] == "slot" and k[2] != "all":
                R[k].parent = res("slot", k[1], "all")
        return R[k]

    def mm(out, lhsT, rhs, start, stop, reads, writes):
        S.add("pe", lambda e, out=out, lhsT=lhsT, rhs=rhs, start=start, stop=stop:
              e.matmul(out, lhsT, rhs, start=start, stop=stop), reads=reads, writes=writes)

    def act(out, in_, func, reads, writes, **kw):
        S.add("act", lambda e, out=out, in_=in_, func=func, kw=kw: e.activation(out=out, in_=in_, func=func, **kw),
              reads=reads, writes=writes)

    def dve(method, reads, writes, *a, **kw):
        S.add("dve", lambda e, method=method, a=a, kw=kw: getattr(e, method)(*a, **kw), reads=reads, writes=writes)

    def dma(q, out, in_, reads, writes, key):
        S.add(q, lambda e, out=out, in_=in_: e.dma_start(out=out, in_=in_), reads=reads, writes=writes,
              dma=True, key=key)

    def load_const(dst, src, name, q="sp"):
        dma(q, dst, src, [], [res(name)], key=name)

    load_const(identf[:], ident_d, "identf")
    load_const(cmat[:], cmat_d, "cmat")
    load_const(lstr[:], lst_d, "lstr")
    load_const(maskT[:], maskT_d, "maskT")
    load_const(onesf[:], onesm_d, "onesf")
    load_const(convw[:], convw_d, "convw")
    load_const(glag[:], glag_d, "glag")
    load_const(flag[:], flag_d, "flag")
    dve("memset", [], [res("gwu")], gwu[:], 0.0)
    dve("memset", [], [res("epsc")], epsc[:, 0:1], EPS)
    dve("memset", [], [res("epsc")], epsc[:, 1:2], 1.0)
    load_const(gwu[0:17, :], gwu_d, "gwu")
    dve("tensor_copy", [res("identf")], [res("identb")], identb[:], identf[:])
    dve("tensor_copy", [res("onesf")], [res("onesb")], onesb[:], onesf[:])
    dve("memset", [], [res("Sst")], Sst[:], 0.0)
    dve("memset", [], [res("Sbf")], Sbf[:], 0.0)
    dve("memset", [], [res("zerob")], zerob[:], 0.0)
    dve("memset", [], [res("phalo")], phalo[:], 0.0)
    dve("memset", [], [res("halo_in")], halo_in[:], 0.0)

    dma("sp", gbuf, g_d[0], [], [res("gbuf")], key="gbuf")
    xv = x_d.rearrange("(tb p) d -> p tb d", p=128)
    for tt in range(NTT):
        dma("sp", x_sb[:, tt * 4:(tt + 1) * 4, :], xv[:, tt * 4:(tt + 1) * 4, :],
            [res("x", tt * 4 - 1)] if tt > 0 else [],
            [res("x", tb) for tb in range(tt * 4, tt * 4 + 4)], key=("xin", tt))

    gcount = [0]

    def load_g(i):
        dma("sp", gbuf, g_d[i], [], [res("gbuf")], key="gbuf")

    slot_ctr = [0]

    def next_slot():
        s = slot_ctr[0] % 3
        slot_ctr[0] += 1
        return s

    CONVERTED = set()

    def slot_bid(parts):
        key = tuple(parts)
        if key not in SLOTKEYS:
            SLOTKEYS[key] = len(SLOTKEYS)
        return SLOTKEYS[key]

    def convert_block(parts):
        bid = slot_bid(parts)
        if bid in CONVERTED:
            return
        CONVERTED.add(bid)
        dma("pool", wsb_d[bid], wslots_d[bid], [], [res("wsb", bid)], key=("cvt", bid))

    def load_slot(s, parts):
        bid = slot_bid(parts)
        if bid in CONVERTED:
            dma("pool", slots[s][:].rearrange("p k c -> p (k c)"), wsb_d[bid], [res("wsb", bid)],
                [res("slot", s, 0), res("slot", s, 1)], key=("slot", s))
            return
        dma("pool", slots[s][:].rearrange("p k c -> p (k c)"), wslots_d[bid],
            [res("x", 15)] if slot_ctr[0] <= 3 else [],
            [res("slot", s, 0), res("slot", s, 1)], key=("slot", s))

    def mixer_keys():
        ks = []
        for c0 in (0, 2, 4, 6):
            ks.append([(0, 256, "w_in", O_CC + c0 * 128), (256, 256, "w_in", O_CU + c0 * 128)])
        for hp in range(2):
            ks.append([(0, 256, "w_in", O_Q + hp * 256), (256, 256, "w_in", O_K + hp * 256)])
            ks.append([(0, 512, "w_in", O_V + hp * 512)])
        for cg in range(2):
            ks.append([(0, 512, "w_in", O_CB + cg * 512)])
        for mg in range(4):
            ks.append([(0, 256, "w_out_conv", mg * 256), (256, 256, "w_in", O_GA + mg * 256)])
        for hp in range(2):
            ks.append([(0, 512, "w_in", O_R + hp * 512)])
        for mg in range(4):
            ks.append([(0, 256, "w_out_gla", mg * 256), (256, 256, "w_in", O_GB + mg * 256)])
        for fh in range(2):
            ks.append([(0, 512, "w_o", fh * 512)])
        return ks

    def rms_stats(tbs, junk_=None, junk_res=None):
        junk_ = junk if junk_ is None else junk_
        junk_res = [res("junk")] if junk_res is None else junk_res
        for tb in tbs:
            act(junk_, x_sb[:, tb, :], AF.Square, [res("x", tb)], junk_res + [res("ss", tb)],
                accum_out=ss[:, tb:tb + 1])
            act(rstd[:, tb:tb + 1], ss[:, tb:tb + 1], AF.Ln, [res("ss", tb), res("epsc")], [res("rstd", tb)],
                scale=1.0 / D, bias=epsc[:, 0:1])
            act(rstd[:, tb:tb + 1], rstd[:, tb:tb + 1], AF.Exp, [res("rstd", tb)], [res("rstd", tb)], scale=-0.5)

    def norm_to_hT(gi, hook=False, scr=None):
        if scr is None:
            scr = dict(junk=junk, junk_res=[res("junk")], gbuf=gbuf, gbuf_res=[res("gbuf")],
                       hn=hnb, hn_res=[[res("hn", 0)], [res("hn", 1)]])
        if gi != 0:
            dma("sp", scr["gbuf"], g_d[gi], [], scr["gbuf_res"], key="gbuf" if scr["gbuf"] is gbuf else "gbuf2")

        def rest_a(tb):
            hn = scr["hn"][tb % 2]
            dve("scalar_tensor_tensor", [res("x", tb), res("rstd", tb)] + scr["gbuf_res"], scr["hn_res"][tb % 2],
                hn, x_sb[:, tb, :], rstd[:, tb:tb + 1], scr["gbuf"], ALU.mult, ALU.mult)

        def rest_b(tb):
            hn = scr["hn"][tb % 2]
            for half in range(2):
                b = (tb % 2) * 2 + half
                pb = ps[b][:].bitcast(BF16)
                for kk in range(4):
                    k = half * 4 + kk
                    S.add("pe", lambda e, o_=pb[:, kk * 128:(kk + 1) * 128], i_=hn[:, k * 128:(k + 1) * 128]:
                          e.transpose(o_, i_, identb[:]), reads=scr["hn_res"][tb % 2] + [res("identb")],
                          writes=[res("ps", b)])
                src = pb[:, 0:512].rearrange("p (a b) -> p a b", a=4)
                dst = hT[:, half * 4:(half + 1) * 4, tb * 128:(tb + 1) * 128]
                if half == 0:
                    act(dst, src, AF.Copy, [res("ps", b)], [res("hT", tb, half)])
                else:
                    dve("tensor_copy", [res("ps", b)], [res("hT", tb, half)], dst, src)

        def step(tb):
            if tb < NTB:
                rms_stats([tb], scr["junk"], scr["junk_res"])
            if 2 <= tb < NTB + 2:
                rest_a(tb - 2)
            if tb >= 3:
                rest_b(tb - 3)
            if tb == NTB - 1:
                step(NTB)
                step(NTB + 1)
                step(NTB + 2)
        if hook:
            return step
        for tb in range(NTB):
            step(tb)

    def hT_reads(k, tt):
        return [res("hT", tb, k // 4) for tb in range(tt * 4, tt * 4 + 4)]

    def ffn_prefetch(fi):
        wgu_name = ("wgu1", "wgu2")[fi]
        pre = {}
        for g0 in range(0, 6, 2):
            s = next_slot()
            pre[g0] = s
            load_slot(s, [(0, 256, wgu_name, g0 * 128), (256, 256, wgu_name, DFF + g0 * 128)])
        return pre

    def ffn(fi, post_tb=None, cvt=None, pre=None):
        wgu_name = ("wgu1", "wgu2")[fi]
        wdn = wdn_d[fi]
        passes = [(0, 6), (6, 12), (12, 18), (18, 22)]
        pair = [0]
        obank = [0]
        for (c0, c1) in passes:
            nchp = c1 - c0
            gslots = {}
            for g0 in range(c0, c1, 2):
                if pre is not None and g0 in pre:
                    gslots[g0] = pre[g0]
                    continue
                s = next_slot()
                gslots[g0] = s
                load_slot(s, [(0, 256, wgu_name, g0 * 128),
                              (256, 256, wgu_name, DFF + g0 * 128)])
            dma("pool", wdn_sb[:, 0:nchp, :], wdn[c0 * 128:c1 * 128, :].rearrange("(j p) c -> p j c", p=128), [],
                [res("wdn")], key="wdn")
            if cvt:
                for _ in range(6):
                    if cvt:
                        convert_block(cvt.pop(0))
            for g0 in range(c0, c1, 2):
                s = gslots[g0]
                for jj in range(2):
                    j = g0 + jj
                    for tt in range(NTT):
                        pr = pair[0] % 2
                        pair[0] += 1
                        bg, bu = ps[pr * 2], ps[pr * 2 + 1]
                        for k in range(8):
                            mm(bg[:], slots[s][:, k, jj * 128:(jj + 1) * 128], hT[:, k, tt * 512:(tt + 1) * 512],
                               k == 0, k == 7, [res("slot", s, 0)] + hT_reads(k, tt), [res("ps", pr * 2)])
                        for k in range(8):
                            mm(bu[:], slots[s][:, k, 256 + jj * 128:256 + (jj + 1) * 128],
                               hT[:, k, tt * 512:(tt + 1) * 512],
                               k == 0, k == 7, [res("slot", s, 1)] + hT_reads(k, tt), [res("ps", pr * 2 + 1)])
                        act(sact[pr], bg[:], AF.Silu, [res("ps", pr * 2)], [res("sact", pr)])
                        dve("tensor_tensor", [res("sact", pr), res("ps", pr * 2 + 1)], [res("actT", j - c0, tt)],
                            actT[:, j - c0, tt * 512:(tt + 1) * 512], sact[pr], bu[:], ALU.mult)
            for tb in range(NTB):
                for fh in range(2):
                    b = 4 + obank[0] % 4
                    obank[0] += 1
                    for jj in range(nchp):
                        mm(ps[b][:], actT[:, jj, tb * 128:(tb + 1) * 128], wdn_sb[:, jj, fh * 512:(fh + 1) * 512],
                           jj == 0, jj == nchp - 1, [res("actT", jj, tb // 4), res("wdn")], [res("ps", b)])
                    xs = x_sb[:, tb, fh * 512:(fh + 1) * 512]
                    dve("scalar_tensor_tensor", [res("ps", b), res("x", tb)], [res("x", tb)],
                        xs, ps[b][:], 0.5, xs, ALU.mult, ALU.add)
                if post_tb is not None and c1 == NCH:
                    post_tb(tb)

    def proj_fm(bank, s, part, col0, tt, ktok=None):
        for k in range(8):
            mm(ps[bank][:], slots[s][:, k, col0:col0 + 128], hT[:, k, tt * 512:(tt + 1) * 512],
               k == 0, k == 7, [res("slot", s, part)] + hT_reads(k, tt), [res("ps", bank)])

    useq = [0]
    eseq = [0]
    trseq = [0]

    def gla_pipe(pre, mid, stepfill=None):
        state_only = False

        def glr_proj(tt):
            for k in range(8):
                mm(ps[2][0:16, :], wglr[:, k, :], hT[:, k, tt * 512:(tt + 1) * 512], k == 0, k == 7,
                   [res("wglr")] + hT_reads(k, tt), [res("ps", 2)])
            act(glrT[0:16, :], ps[2][0:16, :], AF.Copy, [res("ps", 2)], [res("glrT")])

        units = []
        pro = {}
        for tt in range(NTT):
            for hp in range(2):
                for tbi in range(4):
                    u = dict(tt=tt, hp=hp, tbi=tbi, tb=tt * 4 + tbi, n=useq[0], set=hp % 2)
                    useq[0] += 1
                    units.append(u)

        def prologue(tt, hp):
            st_ = hp % 2
            sqk = next_slot()
            load_slot(sqk, [(0, 256, "w_in", O_Q + hp * 256),
                            (256, 256, "w_in", O_K + hp * 256)])
            sv = next_slot()
            load_slot(sv, [(0, 512, "w_in", O_V + hp * 512)])
            pro[(tt, hp)] = sv
            pb = [0]

            def bank():
                b = (2, 5)[pb[0] % 2]
                pb[0] += 1
                return b
            for hh in range(2):
                if not state_only:
                    b = bank()
                    proj_fm(b, sqk, 0, hh * 128, tt)
                    act(qT[st_][hh], ps[b][:], AF.Copy, [res("ps", b)], [res("qT", st_, hh)], scale=float(128 ** -0.5))
                b = bank()
                proj_fm(b, sqk, 1, 256 + hh * 128, tt)
                act(kT[st_][hh], ps[b][:], AF.Copy, [res("ps", b)], [res("kT", st_, hh)])

        def prologue_r(tt, hp):
            srr = next_slot()
            load_slot(srr, [(0, 512, "w_in", O_R + hp * 512)])
            for hh in range(2):
                for vh in range(2):
                    b = (2, 5)[vh]
                    proj_fm(b, srr, 0, hh * 256 + vh * 128, tt)
                    act(sr[hh][vh], ps[b][:], AF.Silu, [res("ps", b)], [res("sr", hh, vh)])
                    dve("tensor_scalar", [res("sr", hh, vh), res("glag")], [res("sr", hh, vh)],
                        sr[hh][vh], sr[hh][vh], glag[:, (hp * 2 + hh) * 2 + vh:(hp * 2 + hh) * 2 + vh + 1], None, ALU.mult)

        def s0(u):
            hp, tb, n = u["hp"], u["tb"], u["n"]
            tok = slice(tb * 128, (tb + 1) * 128)
            ltok = slice(u["tbi"] * 128, (u["tbi"] + 1) * 128)
            sv = pro[(u["tt"], hp)]
            rv, r2 = n % NV, n % 2
            for k in range(8):
                mm(ps[2][:], hT[:, k, tok], slots[sv][:, k, :], k == 0, k == 7,
                   [res("slot", sv, 0), res("hT", tb, k // 4)], [res("ps", 2)])
            act(v_sb[rv], ps[2][:], AF.Copy, [res("ps", 2)], [res("v_sb", rv)])
            mm(ps[3][:, 0:256], glrT[0:17, ltok], gwu[0:17, hp * 256:(hp + 1) * 256], True, True,
               [res("glrT"), res("gwu")], [res("ps", 3)])
            act(e1[r2], ps[3][:, 0:256], AF.Exp, [res("ps", 3)], [res("e1", r2)], scale=-1.0)
            act(spb[r2], e1[r2], AF.Ln, [res("e1", r2), res("epsc")], [res("spb", r2)], bias=epsc[:, 1:2])

        def s1(u):
            n = u["n"]
            r2 = n % 2
            u["E"] = []
            for hh in range(2):
                re_ = eseq[0] % NE
                eseq[0] += 1
                u["E"].append(re_)
                if False:
                    pass
                else:
                    cb_ = 4 if hh == 0 else 1
                    mm(ps[cb_][:], spb[r2][:, hh * 128:(hh + 1) * 128], cmat[:], True, True,
                       [res("spb", r2), res("cmat")], [res("ps", cb_)])
                    act(Eb[re_], ps[cb_][:], AF.Exp, [res("ps", cb_)], [res("Eb", re_)])

        def s1_dve(u):
            n = u["n"]
            for hh in range(2):
                re_ = u["E"][hh]
                dve("tensor_copy", [res("Eb", re_)], [res("dec", n % NDEC, hh)], decb[n % NDEC][:, hh, :],
                    Eb[re_][:, 63:128:64])

        def s2(u):
            n, st_ = u["n"], u["set"]
            ltok = slice(u["tbi"] * 128, (u["tbi"] + 1) * 128)
            rq, rk = n % NQI, n % NTR
            for hh in range(2):
                re_ = u["E"][hh]
                E = Eb[re_]
                dve("tensor_tensor", [res("qT", st_, hh), res("Eb", re_)], [res("qq", rq, hh)],
                    qqb[rq][:, hh], E[:, 0:256].rearrange("p (a t) -> p a t", a=2),
                    qT[st_][hh][:, ltok].unsqueeze(1).broadcast_to([128, 2, 128]), ALU.mult)
                dve("tensor_tensor", [res("kT", st_, hh), res("Eb", re_)], [res("kk", rk, hh)],
                    kkb[rk][:, hh], E[:, 256:512].rearrange("p (a t) -> p a t", a=2),
                    kT[st_][hh][:, ltok].unsqueeze(1).broadcast_to([128, 2, 128]), ALU.mult)

        def s3(u):
            n = u["n"]
            rs_, rq, rk = n % NST, n % NQI, n % NTR
            p5b = ps[5][:].bitcast(BF16)
            for hh in range(2):
                mm(ps[5][:, hh * 128:(hh + 1) * 128], kkb[rk][:, hh, 0, :], qqb[rq][:, hh, 1, :], True, True,
                   [res("kk", rk, hh), res("qq", rq, hh)], [res("ps", 5)])
                S.add("pe", lambda e, o_=p5b[:, 512 + hh * 128:512 + (hh + 1) * 128], i_=kkb[rk][:, hh, 1, :]:
                      e.transpose(o_, i_, identb[:]), reads=[res("kk", rk, hh), res("identb")], writes=[res("ps", 5)])
            dve("tensor_tensor", [res("ps", 5), res("maskT")], [res("sTb", rs_)],
                sTb[rs_], ps[5][:, 0:256].rearrange("p (h t) -> p h t", h=2),
                maskT[:].unsqueeze(1).broadcast_to([128, 2, 128]), ALU.mult)
            act(ksT[rs_], p5b[:, 512:768].rearrange("p (h t) -> p h t", h=2), AF.Copy, [res("ps", 5)],
                [res("ksT", rs_)])

        def s4(u):
            hp, n, st_ = u["hp"], u["n"], u["set"]
            rv, rs_ = n % NV, n % NST
            ob = 7 if n % 2 == 0 else 0
            mm(ps[ob][:], zerob[:], qT[st_][0], True, False, [res("zerob"), res("qT", st_, 0)], [res("ps", ob)])
            for hh in range(2):
                for vh in range(2):
                    oc = (hh * 2 + vh) * 128
                    mm(ps[ob][:, oc:oc + 128], v_sb[rv][:, hh * 256 + vh * 128:hh * 256 + (vh + 1) * 128],
                       sTb[rs_][:, hh, :], False, False, [res("v_sb", rv), res("sTb", rs_)], [res("ps", ob)])
            s4c(u, 0)

        def s4b(u):
            s4c(u, 1)
            r2 = u["n"] % 2
            ob = 7 if u["n"] % 2 == 0 else 0
            act(o_sb[r2], ps[ob][:], AF.Copy, [res("ps", ob)], [res("o_sb", r2)])
            act(sq, ps[ob][:], AF.Square, [res("ps", ob)], [res("sq")])

        def s4c(u, c):
            hp, n = u["hp"], u["n"]
            ob = 7 if n % 2 == 0 else 0
            rv, rs_, rq, rd = n % NV, n % NST, n % NQI, n % NDEC
            cs = slice(c * 64, (c + 1) * 64)
            for hh in range(2):
                h = hp * 2 + hh
                for vh in range(2):
                    oc = (hh * 2 + vh) * 128 + c * 64
                    mm(ps[ob][:, oc:oc + 64], Sbf[:, h, vh * 128:(vh + 1) * 128], qqb[rq][:, hh, 0, cs],
                       False, c == 1 and hh == 1 and vh == 1, [res("Sbf", h), res("qq", rq, hh)],
                       [res("ps", ob)])
            for hh in range(2):
                mm(ps[6][:, hh * 256:(hh + 1) * 256], ksT[rs_][cs, hh, :], v_sb[rv][cs, hh * 256:(hh + 1) * 256],
                   True, True, [res("ksT", rs_), res("v_sb", rv)], [res("ps", 6)])
            for hh in range(2):
                h = hp * 2 + hh
                dve("scalar_tensor_tensor", [res("Sst", h), res("dec", rd, hh), res("ps", 6)], [res("Sbf", h)],
                    Sbf[:, h, :], Sst[:, h, :], decb[rd][:, hh, c:c + 1], ps[6][:, hh * 256:(hh + 1) * 256],
                    ALU.mult, ALU.add)
            for hh in range(2):
                h = hp * 2 + hh
                dve("scalar_tensor_tensor", [res("Sst", h), res("dec", rd, hh), res("ps", 6)], [res("Sst", h)],
                    Sst[:, h, :], Sst[:, h, :], decb[rd][:, hh, c:c + 1], ps[6][:, hh * 256:(hh + 1) * 256],
                    ALU.mult, ALU.add)

        def s5(u):
            hp, n = u["hp"], u["n"]
            r2 = n % 2
            ltok = slice(u["tbi"] * 128, (u["tbi"] + 1) * 128)
            for hh in range(2):
                for vh in range(2):
                    oc = (hh * 2 + vh) * 128
                    mm(ps[3][:, 256 + hh * 128:256 + (hh + 1) * 128], onesb[:], sq[:, oc:oc + 128], vh == 0, vh == 1,
                       [res("onesb"), res("sq")], [res("ps", 3)])
            act(rs, ps[3][:, 256:512], AF.Ln, [res("ps", 3), res("epsc")], [res("rs")], bias=epsc[:, 0:1])
            act(rs, rs, AF.Exp, [res("rs")], [res("rs")], scale=-0.5)

        def s5_dve(u):
            hp, n = u["hp"], u["n"]
            r2 = n % 2
            ltok = slice(u["tbi"] * 128, (u["tbi"] + 1) * 128)
            ov = o_sb[r2].rearrange("p (h v t) -> p h v t", h=2, v=2)
            dve("tensor_tensor", [res("o_sb", r2), res("rs")], [res("o_sb", r2)],
                ov, ov, rs.rearrange("p (h t) -> p h t", h=2).unsqueeze(2).broadcast_to([128, 2, 2, 128]), ALU.mult)
            dve("tensor_tensor", [res("o_sb", r2)] + [res("sr", a // 2, a % 2) for a in range(4)],
                [res("BUF1", hp * 4 + a) for a in range(4)],
                BUF1[:, hp * 4:(hp + 1) * 4, ltok], o_sb[r2].rearrange("p (a t) -> p a t", a=4),
                srall3[:, :, ltok], ALU.mult)

        order = [(s4, 4), (s3, 3), (s5, 5), (s2, 2), (s1, 1), (s4b, 4), (s1_dve, 1), (s5_dve, 5)]
        nst = 6
        nu = len(units)
        pre()
        for step in range(nu + nst - 1):
            for fn_, lag in order:
                ui = step - lag
                if 0 <= ui < nu:
                    fn_(units[ui])
            if step >= 12 and (step - 12) % 8 == 0:
                mid((step - 12) // 8)
            if step < nu:
                u = units[step]
                if u["hp"] == 0 and u["tbi"] == 0:
                    glr_proj(u["tt"])
                if u["tbi"] == 0:
                    prologue(u["tt"], u["hp"])
                s0(u)
            if step - 4 >= 0 and step - 4 < nu and units[step - 4]["tbi"] == 0:
                prologue_r(units[step - 4]["tt"], units[step - 4]["hp"])
            if stepfill is not None:
                stepfill(step)

    def gla_state_tile(tt):
        for k in range(8):
            mm(ps[0][0:16, :], wglr[:, k, :], hT[:, k, tt * 512:(tt + 1) * 512], k == 0, k == 7,
               [res("wglr")] + hT_reads(k, tt), [res("ps", 0)])
        act(glrT[0:16, :], ps[0][0:16, :], AF.Copy, [res("ps", 0)], [res("glrT")])
        spq = [e1[0], e1[1], spb[0], spb[1]]
        spr = [("e1", 0), ("e1", 1), ("spb", 0), ("spb", 1)]
        for hp in range(2):
            st_ = hp % 2
            sqk = next_slot()
            load_slot(sqk, [(0, 256, "w_in", O_Q + hp * 256), (256, 256, "w_in", O_K + hp * 256)])
            sv = next_slot()
            load_slot(sv, [(0, 512, "w_in", O_V + hp * 512)])
            for hh in range(2):
                proj_fm(hh, sqk, 1, 256 + hh * 128, tt)
                act(kT[st_][hh], ps[hh][:], AF.Copy, [res("ps", hh)], [res("kT", st_, hh)])
            for tbi in range(4):
                tb = tt * 4 + tbi
                tok = slice(tb * 128, (tb + 1) * 128)
                ltok = slice(tbi * 128, (tbi + 1) * 128)
                vb = 2 if tbi % 2 == 0 else 7
                for k in range(8):
                    mm(ps[vb][:], hT[:, k, tok], slots[sv][:, k, :], k == 0, k == 7,
                       [res("slot", sv, 0), res("hT", tb, k // 4)], [res("ps", vb)])
                act(v_sb[tbi], ps[vb][:], AF.Copy, [res("ps", vb)], [res("v_sb", tbi)])
                mm(ps[3][:, 0:256], glrT[0:17, ltok], gwu[0:17, hp * 256:(hp + 1) * 256], True, True,
                   [res("glrT"), res("gwu")], [res("ps", 3)])
                act(Eb[0][:, 0:256], ps[3][:, 0:256], AF.Exp, [res("ps", 3)], [res("Eb", 0)], scale=-1.0)
                act(spq[tbi], Eb[0][:, 0:256], AF.Ln, [res("Eb", 0), res("epsc")], [res(*spr[tbi])],
                    bias=epsc[:, 1:2])
            SB, TB_, KB = [4, 1], [5, 0], [6, 3]
            for hh in range(2):
                hc = slice(hh * 128, (hh + 1) * 128)
                for j in range(4):
                    for i in range(3, j - 1, -1):
                        mm(ps[SB[hh]][:, j * 128:(j + 1) * 128], spq[i][:, hc], lstr[:] if i == j else onesf[:],
                           i == 3, i == j, [res(*spr[i]), res("lstr"), res("onesf")], [res("ps", SB[hh])])
                for i in range(4):
                    mm(ps[TB_[hh]][:, 0:2], spq[i][:, hc], onesf[:, 0:2], i == 0, i == 3,
                       [res(*spr[i]), res("onesf")], [res("ps", TB_[hh])])
            for hh in range(2):
                re_ = 1 + hh
                act(Eb[re_], ps[SB[hh]][:], AF.Exp, [res("ps", SB[hh])], [res("Eb", re_)], scale=-16.0)
                act(decb[hh][:, 0, :], ps[TB_[hh]][:, 0:2], AF.Exp, [res("ps", TB_[hh])], [res("dec", hh, 0)],
                    scale=-16.0)
            for hh in range(2):
                re_ = 1 + hh
                dve("tensor_tensor", [res("kT", st_, hh), res("Eb", re_)], [res("qT", st_, hh)],
                    qT[st_][hh], kT[st_][hh], Eb[re_], ALU.mult)
            for hh in range(2):
                ksb = qT[st_][hh]
                pbb = ps[TB_[hh]][:].bitcast(BF16)
                for tbi in range(4):
                    S.add("pe", lambda e, o_=pbb[:, 512 + tbi * 128:512 + (tbi + 1) * 128],
                          i_=ksb[:, tbi * 128:(tbi + 1) * 128]: e.transpose(o_, i_, identb[:]),
                          reads=[res("qT", st_, hh), res("identb")], writes=[res("ps", TB_[hh])])
            for hh in range(2):
                pbb = ps[TB_[hh]][:].bitcast(BF16)
                act(sr[hh][0], pbb[:, 512:1024], AF.Copy, [res("ps", TB_[hh])], [res("sr", hh, 0)])
            for hh in range(2):
                kst = sr[hh][0]
                for tbi in range(4):
                    mm(ps[KB[hh]][:, 0:256], kst[:, tbi * 128:(tbi + 1) * 128],
                       v_sb[tbi][:, hh * 256:(hh + 1) * 256], tbi == 0, tbi == 3,
                       [res("sr", hh, 0), res("v_sb", tbi)], [res("ps", KB[hh])])
            for hh in range(2):
                h = hp * 2 + hh
                dve("scalar_tensor_tensor", [res("Sst", h), res("dec", hh, 0), res("ps", KB[hh])], [res("Sst", h)],
                    Sst[:, h, :], Sst[:, h, :], decb[hh][:, 0, 0:1], ps[KB[hh]][:, 0:256],
                    ALU.mult, ALU.add)

    conv_pre = {}

    def conv_tile(tt):
        pc = [0]
        for cg in range(2):
            sA = []
            for pq in range(2):
                c0 = cg * 4 + pq * 2
                if (tt, cg, pq) in conv_pre:
                    s = conv_pre.pop((tt, cg, pq))
                else:
                    s = next_slot()
                    load_slot(s, [(0, 256, "w_in", O_CC + c0 * 128),
                                  (256, 256, "w_in", O_CU + c0 * 128)])
                for cc_ in range(2):
                    c = c0 + cc_
                    r2 = pc[0] % 2
                    pc[0] += 1
                    proj_fm(0 + r2 * 2, s, 0, cc_ * 128, tt)
                    proj_fm(1 + r2 * 2, s, 1, 256 + cc_ * 128, tt)
                    act(ccs[0], ps[0 + r2 * 2][:], AF.Copy, [res("ps", 0 + r2 * 2)], [res("ccs", 0)])
                    pb = pbuf[r2]
                    dve("tensor_copy", [res("phalo", c)], [res("pbuf", r2)], pb[:, 0:2], phalo[:, c, :])
                    dve("tensor_tensor", [res("ccs", 0), res("ps", 1 + r2 * 2)], [res("pbuf", r2)],
                        pb[:, 2:514], ccs[0], ps[1 + r2 * 2][:], ALU.mult)
                    dve("tensor_copy", [res("pbuf", r2)], [res("phalo", c)], phalo[:, c, :], pb[:, 512:514])
                    act(tsc[r2], pb[:, 2:514], AF.Copy, [res("pbuf", r2), res("convw")], [res("tsc", r2)],
                        scale=convw[:, c, 2:3])
                    dve("scalar_tensor_tensor", [res("pbuf", r2), res("tsc", r2), res("convw")], [res("tsc", r2)],
                        tsc[r2], pb[:, 1:513], convw[:, c, 1:2], tsc[r2], ALU.mult, ALU.add)
                    dve("scalar_tensor_tensor", [res("pbuf", r2), res("tsc", r2), res("convw")], [res("tsc", r2)],
                        tsc[r2], pb[:, 0:512], convw[:, c, 0:1], tsc[r2], ALU.mult, ALU.add)
                    act(BUF1[:, c, :], tsc[r2], AF.Copy, [res("tsc", r2)], [res("BUF1", c)])
            s = next_slot()
            load_slot(s, [(0, 512, "w_in", O_CB + cg * 512)])
            for cc_ in range(4):
                c = cg * 4 + cc_
                b = 4 + cc_ % 2
                proj_fm(b, s, 0, cc_ * 128, tt)
                dve("tensor_tensor", [res("ps", b), res("BUF1", c)], [res("BUF1", c)],
                    BUF1[:, c, :], ps[b][:], BUF1[:, c, :], ALU.mult)

    def outproj_gate(tt, w_name, gate_off, accumulate):
        cnt = 0
        for mg in range(4):
            sx = next_slot()
            load_slot(sx, [(0, 256, w_name, mg * 256), (256, 256, "w_in", gate_off + mg * 256)])
            sy = sx
            for mm_ in range(2):
                m = mg * 2 + mm_
                r2 = cnt % 2
                cnt += 1
                by, bg_ = r2 * 2, r2 * 2 + 1
                for k in range(8):
                    mm(ps[by][:], slots[sx][:, k, mm_ * 128:(mm_ + 1) * 128], BUF1[:, k, :], k == 0, k == 7,
                       [res("slot", sx, 0), res("BUF1", k)], [res("ps", by)])
                proj_fm(bg_, sy, 1, 256 + mm_ * 128, tt)
                act(sga[r2], ps[bg_][:], AF.Sigmoid, [res("ps", bg_)], [res("sga", r2)])
                if not accumulate:
                    dve("tensor_tensor", [res("sga", r2), res("ps", by)], [res("BUF2", m)],
                        BUF2[:, m, :], sga[r2], ps[by][:], ALU.mult)
                else:
                    dve("tensor_tensor", [res("sga", r2), res("ps", by)], [res("sga", r2)],
                        sga[r2], sga[r2], ps[by][:], ALU.mult)
                    dve("tensor_tensor", [res("sga", r2), res("BUF2", m)], [res("BUF2", m)],
                        BUF2[:, m, :], sga[r2], BUF2[:, m, :], ALU.add)

    def wo_tile(tt, post_tb=None):
        cnt = 0
        for fh in range(2):
            s = next_slot()
            load_slot(s, [(0, 512, "w_o", fh * 512)])
            for tbi in range(4):
                tb = tt * 4 + tbi
                b = 4 + cnt % 4
                cnt += 1
                for k in range(8):
                    mm(ps[b][:], BUF2[:, k, tbi * 128:(tbi + 1) * 128], slots[s][:, k, :], k == 0, k == 7,
                       [res("BUF2", k), res("slot", s, 0)], [res("ps", b)])
                xs = x_sb[:, tb, fh * 512:(fh + 1) * 512]
                dve("tensor_tensor", [res("ps", b), res("x", tb)], [res("x", tb)], xs, ps[b][:], xs, ALU.add)
                if post_tb is not None and fh == 1:
                    post_tb(tb)

    halo_slots = []

    def mixer_prefetch():
        dma("pool", wglr[:], wglr_d.rearrange("(k p) c -> p k c", p=128), [], [res("wglr")],
            key="wglr")
        if EXCH:
            for cg in range(3):
                s = next_slot()
                c0 = cg * 2
                load_slot(s, [(0, 256, "w_in", O_CC + c0 * 128), (256, 256, "w_in", O_CU + c0 * 128)])
                halo_slots.append(s)

    def mixer():
        dve("memset", [], [res("glrT")], glrT[:, :], 1.0)
        RG = [[0, 1], [2, 3], [4, 5], [6, 7]]
        if EXCH:
            halo_send = tsc[0]
            for cg in range(4):
                c0 = cg * 2
                if cg < len(halo_slots):
                    s = halo_slots[cg]
                else:
                    s = next_slot()
                    load_slot(s, [(0, 256, "w_in", O_CC + c0 * 128),
                                  (256, 256, "w_in", O_CU + c0 * 128)])
                for cc_ in range(2):
                    c = c0 + cc_
                    for which in range(2):
                        for k in range(8):
                            mm(ps[which][:, 0:2], slots[s][:, k, which * 256 + cc_ * 128:which * 256 + (cc_ + 1) * 128],
                               hT[:, k, T - 2:T], k == 0, k == 7, [res("slot", s, which), res("hT", 15, k // 4)],
                               [res("ps", which)])
                    act(ccs[0][:, 0:2], ps[0][:, 0:2], AF.Copy, [res("ps", 0)], [res("ccs", 0)])
                    dve("tensor_tensor", [res("ccs", 0), res("ps", 1)], [res("halo_send")],
                        halo_send[:, c * 2:c * 2 + 2], ccs[0][:, 0:2], ps[1][:, 0:2], ALU.mult)
            dma("sp", hsend_d, halo_send[:, 0:16], [res("halo_send")], [res("hsend")], key="hsend")
            S.add("pool", lambda e: e.collective_compute(
                "AllGather", ALU.bypass, replica_groups=RG, ins=[hsend_d], outs=[hrecv_d]),
                reads=[res("hsend")], writes=[res("hrecv")], dma=True, key="cch", inc=CC_INC[0])
            dma("sp", halo_in[:].rearrange("p c t -> p (c t)"), hrecv_d[0:128, :], [res("hrecv")],
                [res("halo_in")], key="xr1")
            for tt in range(NTT):
                gla_state_tile(tt)
            dve("tensor_scalar", [res("halo_in"), res("flag")] + [res("phalo", c) for c in range(8)],
                [res("phalo", c) for c in range(8)],
                phalo[:].rearrange("p c t -> p (c t)"), halo_in[:].rearrange("p c t -> p (c t)"), flag[:, 0:1], None,
                ALU.mult)
            s_ = next_slot()
            load_slot(s_, [(0, 256, "w_in", O_CC), (256, 256, "w_in", O_CU)])
            conv_pre[(0, 0, 0)] = s_
            dma("sp", xsend_d, Sst[:].rearrange("p h v -> p (h v)"), [res("Sst", h) for h in range(4)],
                [res("xsend")], key="xsend0")
            S.add("pool", lambda e: e.collective_compute(
                "AllGather", ALU.bypass, replica_groups=RG, ins=[xsend_d], outs=[xrecv_d]),
                reads=[res("xsend")], writes=[res("xrecv")], dma=True, key="cc", inc=CC_INC[0])
            dma("sp", Sst[:].rearrange("p h v -> p (h v)"), xrecv_d[0:128, :],
                [res("xrecv")], [res("Sst", h) for h in range(4)], key="xr0")
        def pre():
            conv_tile(0)
            outproj_gate(0, "w_out_conv", O_GA, False)
            if EXCH:
                for h in range(4):
                    dve("tensor_scalar", [res("flag"), res("Sst", h)], [res("Sst", h)],
                        Sst[:, h, :], Sst[:, h, :], flag[:, 0:1], None, ALU.mult)
                    act(Sbf[:, h, :], Sst[:, h, :], AF.Copy, [res("Sst", h)], [res("Sbf", h)])

        n2 = {}

        def mid(tt):
            outproj_gate(tt, "w_out_gla", O_GB, True)
            wo_tile(tt, n2.get("step") if tt == NTT - 1 else None)
            if tt + 1 < NTT:
                conv_tile(tt + 1)
                outproj_gate(tt + 1, "w_out_conv", O_GA, False)
            if tt == NTT - 2 and STAGE >= 3:
                scr = dict(junk=ccs[0].bitcast(BF16), junk_res=[res("ccs", 0)],
                           gbuf=arena[:, TSC_OFF:TSC_OFF + 2048].bitcast(F32), gbuf_res=[res("tsc", 0), res("tsc", 1)],
                           hn=[pbuf[0].bitcast(BF16)[:, 0:1024], pbuf[1].bitcast(BF16)[:, 0:1024]],
                           hn_res=[[res("pbuf", 0)], [res("pbuf", 1)]])
                n2["step"] = norm_to_hT(2, hook=True, scr=scr)

        def stepfill(step):
            if "step" in n2 and 29 <= step <= 34:
                for tb in ((step - 29) * 2, (step - 29) * 2 + 1):
                    n2["step"](tb)

        gla_pipe(pre, mid, stepfill)

    def final_out(do_norm, hook=False):
        if do_norm:
            load_g(3)

        def fin(tb):
            st = ostage[tb % NOST]
            if do_norm:
                dve("scalar_tensor_tensor", [res("x", tb), res("rstd", tb), res("gbuf")], [res("ostage", tb % NOST)],
                    st, x_sb[:, tb, :], rstd[:, tb:tb + 1], gbuf, ALU.mult, ALU.mult)
            else:
                dve("tensor_copy", [res("x", tb)], [res("ostage", tb % NOST)], st, x_sb[:, tb, :])
            dma("sp", out_d[tb * 128:(tb + 1) * 128, :], st, [res("ostage", tb % NOST)], [res("outd", tb % NOST)],
                key=("out", tb % NOST))

        def step(tb):
            if do_norm:
                rms_stats([tb])
            if tb >= 1:
                fin(tb - 1)
            if tb == NTB - 1:
                fin(tb)
        if hook:
            return step
        for tb in range(NTB):
            step(tb)

    norm_to_hT(0)
    if STAGE >= 2:
        ffn(0, norm_to_hT(1, hook=True), cvt=mixer_keys())
    else:
        ffn(0)
    if STAGE >= 2:
        mixer_prefetch()
        S.barrier()
        mixer()
    if STAGE >= 3:
        pre2 = ffn_prefetch(1)
        S.barrier()
        ffn(1, final_out(STAGE >= 4, hook=True), pre=pre2)
    else:
        final_out(STAGE >= 4)

    keys = S.finalize()
    sem_cms = {}
    for e in ENGS:
        cm = nc.semaphore(f"sem_{e}")
        sem_cms[e] = cm.__enter__()
        es.append(cm)
    for i, k in enumerate(keys):
        cm = nc.semaphore(f"semk_{i}")
        sem_cms[("k", k)] = cm.__enter__()
        es.append(cm)

    def sem_of(op):
        return sem_cms[("k", op.key)] if op.dma else sem_cms[op.eng]

    def emit(eng_name, e):
        waited = {}
        for op in S.streams[eng_name]:
            need = {}
            for d in op.deps:
                sm = sem_of(d)
                if need.get(sm, (0,))[0] < d.val:
                    need[sm] = (d.val, sm)
            for sm, (val, _) in need.items():
                if waited.get(sm, 0) < val:
                    e.wait_ge(sm, val)
                    waited[sm] = val
            ins = op.fn(e)
            if op.sig:
                ins.then_inc(sem_of(op), op.inc if op.dma else 1)
        if eng_name == "sp":
            for k in keys:
                if isinstance(k, tuple) and k[0] == "out":
                    tot = sum(o_.inc for o_ in S.streams["sp"] if o_.dma and o_.key == k)
                    e.wait_ge(sem_cms[("k", k)], tot)

    with nc.Block() as block:
        @block.tensor
        def _(e):
            emit("pe", e)

        @block.scalar
        def _(e):
            emit("act", e)

        @block.vector
        def _(e):
            emit("dve", e)

        @block.gpsimd
        def _(e):
            emit("pool", e)

        @block.sync
        def _(e):
            emit("sp", e)

    for cm in reversed(es):
        cm.__exit__(None, None, None)
    return nc


CC_INC = [1]


def _consts():
    s = np.arange(128)[:, None]
    t = np.arange(128)[None, :]
    same = (s // 64) == (t // 64)
    tri = (same & (s <= t)).astype(np.float64)
    ref = (t // 64) * 64 + 31
    last = (t // 64) * 64 + 63
    tri_ref = (same & (s <= ref)).astype(np.float64)
    tri_last = (same & (s <= last)).astype(np.float64)
    Dm = tri - tri_ref
    Lm = tri_last - tri
    cm = np.concatenate([tri, Dm, -Dm, Lm], axis=1) * (-1.0 / 16.0)
    cm1 = (s > t).astype(np.float64) / 256.0
    maskT = (same & (s <= t)).astype(np.float32)
    return (np.eye(128, dtype=np.float32), cm.astype(np.float32), maskT,
            np.full((128, 128), 1.0 / 256.0, dtype=np.float32), cm1.astype(np.float32))


def kernel(x, norm_ffn1_g, ffn1_w_gu, ffn1_w_down, norm_mix_g, w_in, conv_w, gate_w_up, gate_b,
           gla_norm_g, w_out_conv, w_out_gla, w_o, norm_ffn2_g, ffn2_w_gu, ffn2_w_down, final_norm_g):
    f = lambda a: np.ascontiguousarray(np.asarray(a, dtype=np.float32))
    x = f(x)
    ident, cmat, maskT, onesm, cmat1 = _consts()
    rep = lambda g: np.ascontiguousarray(np.broadcast_to(f(g).reshape(1, D), (128, D)))
    shared = {
        "g1": rep(norm_ffn1_g), "gm": rep(norm_mix_g), "g2": rep(norm_ffn2_g), "gf": rep(final_norm_g),
        "wdn1": f(ffn1_w_down).reshape(DFF, D), "wdn2": f(ffn2_w_down).reshape(DFF, D),
        "convw": np.ascontiguousarray(f(conv_w).reshape(3, 8, 128).transpose(2, 1, 0)),
        "glag": np.ascontiguousarray(f(gla_norm_g).reshape(8, 128).T),
        "gwu": np.ascontiguousarray(np.concatenate([f(gate_w_up).reshape(16, 512), f(gate_b).reshape(1, 512)], axis=0)),
        "ident": ident, "cmat": cmat, "lstrict": cmat1, "maskT": maskT, "onesm": onesm,
    }
    nc = build_nc()
    W = {"wgu1": f(ffn1_w_gu).reshape(D, 2 * DFF), "wgu2": f(ffn2_w_gu).reshape(D, 2 * DFF),
         "w_in": f(w_in).reshape(D, INC), "w_out_conv": f(w_out_conv).reshape(D, D),
         "w_out_gla": f(w_out_gla).reshape(D, D), "w_o": f(w_o).reshape(D, D)}
    assert len(SLOTKEYS) <= NSLOTS, len(SLOTKEYS)
    wslots = np.empty((NSLOTS, 128, 8, 512), dtype=np.float32)
    for key, bid in SLOTKEYS.items():
        for (co, n, name, c0) in key:
            wslots[bid][:, :, co:co + n] = W[name][:, c0:c0 + n].reshape(8, 128, n).transpose(1, 0, 2)
    shared["wslots"] = wslots.reshape(NSLOTS, 128, 4096)
    shared["wglr_in"] = np.ascontiguousarray(W["w_in"][:, O_G:O_G + 16])
    in_maps = []
    for c in range(8):
        b, half = c // 2, c % 2
        m = dict(shared)
        m["x"] = np.ascontiguousarray(x[b, half * T:(half + 1) * T, :])
        m["flag"] = np.full((128, 1), float(half), dtype=np.float32)
        in_maps.append(m)
    res = run_bass_kernel_spmd(nc, in_maps, core_ids=list(range(8)))
    out = np.empty((4, 2 * T, D), dtype=np.float32)
    for c in range(8):
        b, half = c // 2, c % 2
        out[b, half * T:(half + 1) * T, :] = np.asarray(res.results[c]["out"], dtype=np.float32)
    return out
```
